# Optimizing a Trainium2 kernel written in Bass

```python
import math
import jax, jax.numpy as jnp
from jax import lax
import numpy as np

D_MODEL = 1024
BATCH = 8
SEQ = 4096
DEPTH = 4

D_BRANCH = D_MODEL
N_BRANCH = 3
CONV_K = 31
HGRN_HEAD = 128
HGRN_HEADS = D_BRANCH // HGRN_HEAD
HGRN_CHUNK = 64
FOX_HEAD = 64
FOX_HEADS = D_BRANCH // FOX_HEAD
Q_BLOCK = 128
EPS = 1e-6

SPLIT_SIZES = (D_BRANCH, D_BRANCH, D_BRANCH,
               D_BRANCH, D_BRANCH, D_BRANCH, D_BRANCH,
               D_BRANCH, D_BRANCH, D_BRANCH, D_BRANCH, FOX_HEADS,
               N_BRANCH * D_MODEL)
N_IN = sum(SPLIT_SIZES)

kernel_name = 'hybrid_conv_hgrn2_fox_gated'


def rms_norm(x, g):
    xf = x.astype(jnp.float32)
    y = xf * lax.rsqrt(jnp.mean(xf * xf, axis=-1, keepdims=True) + EPS)
    return (y * g.astype(jnp.float32)).astype(x.dtype)


def layer_norm(x, g, b):
    xf = x.astype(jnp.float32)
    mu = jnp.mean(xf, axis=-1, keepdims=True)
    var = jnp.mean(jnp.square(xf - mu), axis=-1, keepdims=True)
    y = (xf - mu) * lax.rsqrt(var + EPS)
    return (y * g.astype(jnp.float32) + b.astype(jnp.float32)).astype(x.dtype)


def conformer_conv(val, glu_gate, conv_w, conv_b, ln_g, ln_b):
    u = val * jax.nn.sigmoid(glu_gate)
    y = lax.conv_general_dilated(
        u, conv_w[:, None, :].astype(u.dtype), window_strides=(1,),
        padding=[(CONV_K - 1, 0)],
        dimension_numbers=('NWC', 'WIO', 'NWC'),
        feature_group_count=D_BRANCH) + conv_b
    return jax.nn.silu(layer_norm(y, ln_g, ln_b))


def hgrn2(q, f_pre, i, lb, norm_g):
    B, S, _ = q.shape
    H, Dh, C = HGRN_HEADS, HGRN_HEAD, HGRN_CHUNK
    nc = S // C
    f32 = jnp.float32
    qf = jax.nn.silu(q.astype(f32))
    z = f_pre.astype(f32)
    lbf = lb.astype(f32)
    f = lbf + (1.0 - lbf) * jax.nn.sigmoid(z)
    k = 1.0 - f
    log_f = jnp.logaddexp(jnp.log(lbf), jnp.log1p(-lbf) + jax.nn.log_sigmoid(z))
    v = i.astype(f32)

    def to_chunks(t):
        return t.reshape(B, nc, C, H, Dh).transpose(1, 0, 3, 2, 4)

    causal = jnp.tril(jnp.ones((C, C), dtype=bool))

    def step(state, xs):
        qc, kc, vc, gc = xs
        b = jnp.cumsum(gc, axis=2)
        diff = b[:, :, :, None, :] - b[:, :, None, :, :]
        decay = jnp.exp(jnp.where(causal[:, :, None], diff, -jnp.inf))
        scores = jnp.einsum('bhtd,bhsd,bhtsd->bhts', qc, kc, decay)
        o = (jnp.einsum('bhts,bhsv->bhtv', scores, vc)
             + jnp.einsum('bhtd,bhdv->bhtv', qc * jnp.exp(b), state))
        b_last = b[:, :, -1:, :]
        new_state = (jnp.exp(b_last[:, :, 0, :])[..., None] * state
                     + jnp.einsum('bhsd,bhsv->bhdv', kc * jnp.exp(b_last - b), vc))
        return new_state, o

    s0 = jnp.zeros((B, H, Dh, Dh), f32)
    _, o = lax.scan(step, s0, (to_chunks(qf), to_chunks(k), to_chunks(v), to_chunks(log_f)))
    o = o.transpose(1, 0, 3, 2, 4).reshape(B, S, H, Dh)
    o = o * lax.rsqrt(jnp.mean(o * o, axis=-1, keepdims=True) + EPS)
    o = o * norm_g.astype(f32).reshape(H, Dh)
    return o.reshape(B, S, D_BRANCH).astype(q.dtype)


def forgetting_attention(q, k, v, f_logit, qn_g, kn_g):
    B, S, _ = q.shape
    H, Dh = FOX_HEADS, FOX_HEAD
    f32 = jnp.float32
    qh = rms_norm(q.reshape(B, S, H, Dh), qn_g).transpose(0, 2, 1, 3)
    kh = rms_norm(k.reshape(B, S, H, Dh), kn_g).transpose(0, 2, 1, 3)
    vh = v.reshape(B, S, H, Dh).transpose(0, 2, 1, 3)
    F = jnp.cumsum(jax.nn.log_sigmoid(f_logit.astype(f32)), axis=1).transpose(0, 2, 1)
    nb = S // Q_BLOCK
    q_blocks = qh.reshape(B, H, nb, Q_BLOCK, Dh).transpose(2, 0, 1, 3, 4)
    F_blocks = F.reshape(B, H, nb, Q_BLOCK).transpose(2, 0, 1, 3)
    key_pos = jnp.arange(S)
    scale = 1.0 / math.sqrt(Dh)

    def block(args):
        qb, Fb, blk = args
        logits = jnp.einsum('bhqd,bhkd->bhqk', qb, kh).astype(f32) * scale
        logits = logits + Fb[..., None] - F[:, :, None, :]
        q_pos = blk * Q_BLOCK + jnp.arange(Q_BLOCK)
        logits = jnp.where(key_pos[None, :] <= q_pos[:, None], logits, -jnp.inf)
        p = jax.nn.softmax(logits, axis=-1).astype(vh.dtype)
        return jnp.einsum('bhqk,bhkd->bhqd', p, vh)

    out = lax.map(block, (q_blocks, F_blocks, jnp.arange(nb)))
    return out.transpose(1, 0, 3, 2, 4).reshape(B, S, D_BRANCH)


def setup_inputs(seed: int = 0) -> dict:
    key = jax.random.key(seed)
    ks = jax.random.split(key, 15)
    nrm = jax.random.normal
    return {
        'x': nrm(ks[0], (BATCH, SEQ, D_MODEL), jnp.float32),
        'norm_g': 1.0 + 0.02 * nrm(ks[1], (DEPTH, D_MODEL), jnp.float32),
        'w_in': nrm(ks[2], (DEPTH, D_MODEL, N_IN), jnp.float32) * D_MODEL ** -0.5,
        'conv_w': nrm(ks[3], (DEPTH, CONV_K, D_BRANCH), jnp.float32) * CONV_K ** -0.5,
        'conv_b': 0.02 * nrm(ks[4], (DEPTH, D_BRANCH), jnp.float32),
        'conv_ln_g': 1.0 + 0.02 * nrm(ks[5], (DEPTH, D_BRANCH), jnp.float32),
        'conv_ln_b': 0.02 * nrm(ks[6], (DEPTH, D_BRANCH), jnp.float32),
        'hgrn_lb_logits': 0.5 * nrm(ks[7], (DEPTH, D_BRANCH), jnp.float32),
        'hgrn_norm_g': 1.0 + 0.02 * nrm(ks[8], (DEPTH, D_BRANCH), jnp.float32),
        'fox_f_bias': 2.0 + 0.1 * nrm(ks[9], (DEPTH, FOX_HEADS), jnp.float32),
        'fox_qn_g': 1.0 + 0.02 * nrm(ks[10], (DEPTH, FOX_HEAD), jnp.float32),
        'fox_kn_g': 1.0 + 0.02 * nrm(ks[11], (DEPTH, FOX_HEAD), jnp.float32),
        'w_branch': nrm(ks[12], (DEPTH, N_BRANCH, D_BRANCH, D_MODEL), jnp.float32) * D_BRANCH ** -0.5,
        'w_out': nrm(ks[13], (DEPTH, D_MODEL, D_MODEL), jnp.float32) * (0.5 * D_MODEL ** -0.5),
    }


def reference(x, norm_g, w_in, conv_w, conv_b, conv_ln_g, conv_ln_b, hgrn_lb_logits,
              hgrn_norm_g, fox_f_bias, fox_qn_g, fox_kn_g, w_branch, w_out):
    B, S, _ = x.shape
    split_points = np.cumsum(SPLIT_SIZES)[:-1].tolist()
    p = jax.nn.softmax(hgrn_lb_logits.astype(jnp.float32), axis=0)
    cum = jnp.cumsum(p, axis=0)
    lower_bounds = cum - cum[0:1]
    for l in range(DEPTH):
        h = rms_norm(x, norm_g[l])
        proj = jnp.einsum('bsd,dn->bsn', h, w_in[l])
        (a_val, a_glu, a_gate, b_q, b_f, b_i, b_gate,
         c_q, c_k, c_v, c_gate, c_f, merge) = jnp.split(proj, split_points, axis=-1)
        ya = conformer_conv(a_val, a_glu, conv_w[l], conv_b[l], conv_ln_g[l], conv_ln_b[l]) * jax.nn.silu(a_gate)
        yb = hgrn2(b_q, b_f, b_i, lower_bounds[l], hgrn_norm_g[l]) * jax.nn.silu(b_gate)
        yc = forgetting_attention(c_q, c_k, c_v, c_f + fox_f_bias[l], fox_qn_g[l], fox_kn_g[l]) * jax.nn.silu(c_gate)
        branches = jnp.stack([ya, yb, yc], axis=2)
        branch_d = jnp.einsum('bskc,kcd->bskd', branches, w_branch[l])
        gates = jax.nn.sigmoid(merge.reshape(B, S, N_BRANCH, D_MODEL))
        mixed = jnp.sum(gates * branch_d, axis=2)
        x = x + jnp.einsum('bsd,de->bse', mixed, w_out[l])
    return x
```

```python
import numpy as np
from contextlib import ExitStack
import concourse.bass as bass
import concourse.mybir as mybir
from concourse.bass_utils import run_bass_kernel_spmd

F32 = mybir.dt.float32
BF16 = mybir.dt.bfloat16
AF = mybir.ActivationFunctionType
ALU = mybir.AluOpType

D = 1024
KT = 8
CONV_K = 31
EPS = 1e-6
NT = 144
NG = NT // 4
P1_TILES = 128
V_NG, V_CB, V_LG, V_LB, V_HG, V_LBL, V_CW, V_QG, V_KG, V_FB, NVL = 0, 8, 16, 24, 32, 40, 48, 296, 297, 298, 300


class SemC:
    def __init__(self, h, is_dma=False):
        self.h = h
        self.v = 0
        self.is_dma = is_dma


class Buf:
    def __init__(self, name, dsem=None):
        self.name = name
        self.w = None
        self.r = {}
        self.dsem = dsem


class Eng:
    def __init__(self, name, sem):
        self.name = name
        self.done = SemC(sem)
        self.ops = []
        self.waited = {}


class Prog:
    def __init__(self, nc, stack):
        self.nc = nc
        self.stack = stack
        self.engs = {}
        for n in ("tensor", "vector", "scalar", "gpsimd", "sync"):
            self.engs[n] = Eng(n, stack.enter_context(nc.semaphore("done_" + n)))
        self.dsems = []
        self.nops = 0

    def buf(self, name, dma=False):
        dsem = None
        if dma:
            dsem = SemC(self.stack.enter_context(self.nc.semaphore("d_" + name)), True)
            self.dsems.append(dsem)
        return Buf(name, dsem)

    def _wait(self, e, deps):
        for sc, val in deps:
            if sc.is_dma:
                val = sc.v
            if e.waited.get(sc, 0) < val:
                e.waited[sc] = val
                h = sc.h
                e.ops.append(lambda eng, h=h, val=val: eng.wait_ge(h, val))

    def _deps(self, e, reads, writes, skip_self=False):
        deps = []
        for b in reads:
            if b.w is not None:
                deps.append(b.w)
        for b in writes:
            if b.w is not None:
                deps.append(b.w)
            deps.extend(b.r.items())
        if skip_self:
            deps = [d for d in deps if d[0] is not e.done]
        return deps

    def op(self, en, fn, reads=(), writes=(), inc=True):
        e = self.engs[en]
        self.nops += 1
        self._wait(e, self._deps(e, reads, writes, skip_self=(en == "tensor")))
        if inc:
            e.done.v += 1
            tokv = e.done.v
            h = e.done.h
            e.ops.append(lambda eng, fn=fn, h=h: fn(eng).then_inc(h, 1))
        else:
            tokv = e.done.v + 1
            e.ops.append(lambda eng, fn=fn: fn(eng))
        for b in reads:
            b.r[e.done] = tokv
        for b in writes:
            b.w = (e.done, tokv)
            b.r = {}

    def dma(self, en, out_ap, in_ap, reads=(), writes=(), sem=None, **kw):
        e = self.engs[en]
        self.nops += 1
        if sem is None:
            for b in list(writes) + list(reads):
                if b.dsem is not None:
                    sem = b.dsem
                    break
        assert sem is not None
        self._wait(e, self._deps(e, reads, writes))
        sem.v += 16
        h = sem.h
        e.ops.append(lambda eng, h=h, o=out_ap, i=in_ap, kw=kw:
                     eng.dma_start(out=o, in_=i, **kw).then_inc(h, 16))
        for b in reads:
            b.r[sem] = sem.v
        for b in writes:
            b.w = (sem, sem.v)
            b.r = {}

    def wait_all(self, en, bufs):
        e = self.engs[en]
        self._wait(e, [b.w for b in bufs if b.w is not None])

    def barrier(self):
        toks = [(e.done, e.done.v) for e in self.engs.values() if e.done.v > 0]
        toks += [(s, s.v) for s in self.dsems if s.v > 0]
        for e in self.engs.values():
            self._wait(e, [t for t in toks if t[0] is not e.done])

    def emit(self):
        with self.nc.Block() as block:
            for n, e in self.engs.items():
                if not e.ops:
                    continue

                def body(eng, ops=e.ops):
                    for f in ops:
                        f(eng)
                getattr(block, n)(body)


def build(T, L, dbg=False):
    NCH = T // 512
    NTT = T // 128
    nc = bass.Bass("TRN2", target_bir_lowering=False)

    def dram(name, shape, dt, kind="Internal"):
        return nc.dram_tensor(name, shape, dt, kind=kind).ap()

    xT_in = dram("xT", [D, T], F32, "ExternalInput")
    wst = dram("wst", [L * NT * 128, 1024], F32, "ExternalInput")
    wcf = dram("wcf", [L * 128, 128], F32, "ExternalInput")
    vec_d = dram("vec", [128, L * NVL], F32, "ExternalInput")
    outT = dram("outT", [D, T], F32, "ExternalOutput")
    wbf = dram("wbf", [L * NT * 128, 1024], BF16)
    xs = [dram("xs%d" % i, [D, T], F32) for i in range(2)]
    qn_d = dram("qn_d", [D, T], BF16)
    kn_d = dram("kn_d", [D, T], BF16)
    sc_d = dram("sc_d", [D, T], BF16)
    v_d = dram("v_d", [T, 1536], BF16)
    yc_d = dram("yc_d", [D, T], BF16)
    gC_d = dram("gC_d", [D, T], F32)
    mAB_d = dram("mAB_d", [D, T], F32)
    F_d = dram("F_d", [16, T], F32)
    Fs_d = dram("Fs_d", [16, 3, T], BF16)
    dbg_d = {}
    if dbg:
        dbg_w0 = dram("dbg_w0", [8 * 128, 1024], F32, "ExternalOutput")
        dbg_w1 = dram("dbg_w1", [8 * 128, 1024], F32, "ExternalOutput")
        for n in ("ya", "yb", "yc"):
            dbg_d[n] = dram("dbg_" + n, [D, T], F32, "ExternalOutput")

    fm = lambda ap: ap.rearrange("(k p) t -> p k t", p=128)

    with ExitStack() as st:
        P = Prog(nc, st)
        off = [16384]

        def sb_at(name, shape, dt, o=None):
            nb = int(np.prod(shape[1:])) * (4 if dt == F32 else 2)
            nb = (nb + 31) // 32 * 32
            if o is None:
                o = off[0]
                off[0] += nb
            assert o + nb <= 229120, (name, o, nb)
            return nc.alloc_sbuf_tensor_at(name, list(shape), dt, offset=o)

        psum = [st.enter_context(nc.psum_tensor("ps%d" % i, [128, 512], F32)) for i in range(7)]
        psb = [P.buf("ps%d" % i) for i in range(7)]
        pstr = st.enter_context(nc.psum_tensor("pstr", [128, 128], BF16))
        pstr_b = P.buf("pstr")
        prot = [0]

        def ps_rot(n=5):
            i = prot[0] % n
            prot[0] += 1
            return psum[i], psb[i]

        vec = sb_at("vec", [128, L * NVL], F32)
        vec_b = P.buf("vec", dma=True)
        ones_f = sb_at("ones_f", [128, 128], F32)
        blk_f = sb_at("blk_f", [128, 128], F32)
        id_f = sb_at("id_f", [128, 128], F32)
        id_b = sb_at("id_b", [128, 128], BF16)
        onesK = sb_at("onesK", [128, 128], BF16)
        maskB = sb_at("maskB", [128, 64], F32)
        rmask = sb_at("rmask", [128, 512], BF16)
        onesrow = sb_at("onesrow", [128, 512], BF16)
        lbt = sb_at("lbt", [128, L, 3, 8], F32)
        cb = P.buf("consts")
        wcf_b = sb_at("wcf_b", [128, L, 128], BF16)
        wcf_buf = P.buf("wcf_b", dma=True)
        cw_b = sb_at("cw_b", [128, L, 8, CONV_K], BF16)
        P.dma("sync", vec[:], vec_d, writes=[vec_b])
        for l in range(L):
            P.dma("gpsimd", wcf_b[:, l, :], wcf[l * 128:(l + 1) * 128, :], writes=[wcf_buf])
        G = lambda fn, R=(), W=(): P.op("gpsimd", fn, reads=R, writes=W)
        G(lambda e: e.memset(ones_f[:], 1.0), W=[cb])
        G(lambda e: e.memset(blk_f[:], 0.0), W=[cb])
        G(lambda e: e.memset(blk_f[0:64, 0:64], 1.0), W=[cb])
        G(lambda e: e.memset(blk_f[64:128, 64:128], 1.0), W=[cb])
        G(lambda e: e.memset(id_f[:], 1.0), W=[cb])
        G(lambda e: e.affine_select(out=id_f[:], in_=id_f[:], pattern=[[-1, 128]], compare_op=ALU.is_equal,
                                    fill=0.0, base=0, channel_multiplier=1), R=[cb], W=[cb])
        G(lambda e: e.tensor_copy(out=id_b[:], in_=id_f[:]), R=[cb], W=[cb])
        G(lambda e: e.memset(onesK[:], 0.0), W=[cb])
        G(lambda e: e.memset(onesK[0:3, :], 1.0), W=[cb])
        G(lambda e: e.memset(maskB[:], 1.0), W=[cb])
        G(lambda e: e.affine_select(out=maskB[0:64, :], in_=maskB[0:64, :], pattern=[[1, 64]], compare_op=ALU.is_ge,
                                    fill=0.0, base=0, channel_multiplier=-1), R=[cb], W=[cb])
        G(lambda e: e.affine_select(out=maskB[64:128, :], in_=maskB[64:128, :], pattern=[[1, 64]], compare_op=ALU.is_ge,
                                    fill=0.0, base=0, channel_multiplier=-1), R=[cb], W=[cb])
        maskN = sb_at("maskN", [128, 128], BF16)
        G(lambda e: e.memset(maskN[:], -30000.0), W=[cb])
        G(lambda e: e.affine_select(out=maskN[:], in_=maskN[:], pattern=[[-1, 128]], compare_op=ALU.is_ge,
                                    fill=0.0, base=-1, channel_multiplier=1), R=[cb], W=[cb])
        G(lambda e: e.memset(rmask[:], -1.0), W=[cb])
        G(lambda e: e.memset(rmask[:].rearrange("p (c t) -> p c t", t=64)[:, :, 0:1], 0.0), W=[cb])
        G(lambda e: e.memset(onesrow[:], 1.0), W=[cb])
        Eall = sb_at("Eall", [128, L, 8], F32)
        Esum = sb_at("Esum", [128, 8], F32)
        Ecum = sb_at("Ecum", [128, 8], F32)
        for l in range(L):
            P.op("scalar", lambda e, l=l: e.activation(out=Eall[:, l, :], in_=vec[:, l * NVL + V_LBL:l * NVL + V_LBL + 8],
                                                       func=AF.Exp), reads=[vec_b], writes=[cb])
        VV = lambda fn, R=(), W=(): P.op("vector", fn, reads=R, writes=W)
        VV(lambda e: e.tensor_copy(out=Esum[:], in_=Eall[:, 0, :]), R=[cb], W=[cb])
        for l in range(1, L):
            VV(lambda e, l=l: e.tensor_tensor(out=Esum[:], in0=Esum[:], in1=Eall[:, l, :], op=ALU.add), R=[cb], W=[cb])
        VV(lambda e: e.reciprocal(out=Esum[:], in_=Esum[:]), R=[cb], W=[cb])
        VV(lambda e: e.memset(Ecum[:], 0.0), W=[cb])
        for l in range(L):
            if l > 0:
                VV(lambda e, l=l: e.tensor_tensor(out=Ecum[:], in0=Ecum[:], in1=Eall[:, l, :], op=ALU.add), R=[cb], W=[cb])
            VV(lambda e, l=l: e.tensor_tensor(out=lbt[:, l, 0, :], in0=Ecum[:], in1=Esum[:], op=ALU.mult), R=[cb], W=[cb])
            VV(lambda e, l=l: e.tensor_scalar(out=lbt[:, l, 1, :], in0=lbt[:, l, 0, :], scalar1=-1.0, scalar2=1.0,
                                              op0=ALU.mult, op1=ALU.add), R=[cb], W=[cb])
            VV(lambda e, l=l: e.tensor_scalar(out=lbt[:, l, 2, :], in0=lbt[:, l, 1, :], scalar1=-1.0, scalar2=None,
                                              op0=ALU.mult), R=[cb], W=[cb])
            VV(lambda e, l=l: e.tensor_copy(
                out=cw_b[:, l, :, :],
                in_=vec[:, l * NVL + V_CW:l * NVL + V_CW + 8 * CONV_K].rearrange("p (k j) -> p k j", j=CONV_K)),
               R=[vec_b], W=[cb])
        CONST_END = off[0]

        cvb = {}
        CV = 48
        for l in range(L):
            sizes = [4, 4, 8, 16, 16, 48, 48] if l == 0 else [CV] * (NT // CV)
            j0 = 0
            for sz in sizes:
                b = P.buf("cv%d_%d" % (l, j0), dma=True)
                r0 = (l * NT + j0) * 128
                P.dma("gpsimd", wbf[r0:r0 + sz * 128, :], wst[r0:r0 + sz * 128, :], writes=[b])
                for j in range(j0, j0 + sz):
                    cvb[(l, j // 4)] = b
                j0 += sz
            assert j0 == NT

        NWB = 3

        class WStream:
            def __init__(self):
                self.seq = []
                self.pos = 0
                self.loaded = 0
                self.bufs = None

            def set_bufs(self, tiles, bufs):
                self.tiles, self.bufs = tiles, bufs

            def _load(self, gi):
                l, g = self.seq[gi]
                t, b = self.tiles[gi % NWB], self.bufs[gi % NWB]
                r0 = (l * NT + g * 4) * 128
                src = wbf[r0:r0 + 512, :].rearrange("(j p) n -> p j n", p=128)
                P.dma("sync", t[:].rearrange("p j k n -> p j (k n)"), src, reads=[cvb[(l, g)]], writes=[b])

            def _ensure(self, gi, ahead=NWB - 1):
                while self.loaded <= min(gi + ahead, len(self.seq) - 1):
                    self._load(self.loaded)
                    self.loaded += 1

            def next(self):
                gi, j = divmod(self.pos, 4)
                self._ensure(gi)
                self.pos += 1
                return self.tiles[gi % NWB][:, j, :, :], self.bufs[gi % NWB]

            def next_group(self, ahead=NWB - 1):
                assert self.pos % 4 == 0
                gi = self.pos // 4
                self._ensure(gi, ahead)
                self.pos += 4
                return self.tiles[gi % NWB], self.bufs[gi % NWB]

        ws = WStream()
        for l in range(L):
            for ci in range(NCH):
                ws.seq += [(l, g) for g in range(P1_TILES // 4)]
            for ci in range(NCH):
                ws.seq += [(l, g) for g in range(P1_TILES // 4, NG)]

        def proj(wt, wb, rhs, rb, ps, pb):
            for k in range(KT):
                P.op("tensor", lambda e, k=k: e.matmul(ps[:], wt[:, k, :], rhs[:, k, :], start=(k == 0), stop=(k == KT - 1)),
                     reads=[wb, rb], writes=[pb], inc=(k == KT - 1))

        def act(out, in_, func, R, W, scale=1.0, bias=0.0):
            P.op("scalar", lambda e: e.activation(out=out, in_=in_, func=func, scale=scale, bias=bias), reads=R, writes=W)

        def tt(en, out, in0, in1, op, R, W):
            P.op(en, lambda e: e.tensor_tensor(out=out, in0=in0, in1=in1, op=op), reads=R, writes=W)

        def ts(en, out, in0, s1, op0, R, W, s2=None, op1=None):
            if op1 is None:
                P.op(en, lambda e: e.tensor_scalar(out=out, in0=in0, scalar1=s1, scalar2=None, op0=op0), reads=R, writes=W)
            else:
                P.op(en, lambda e: e.tensor_scalar(out=out, in0=in0, scalar1=s1, scalar2=s2, op0=op0, op1=op1),
                     reads=R, writes=W)

        def stt(out, in0, scalar, in1, op0, op1, R, W):
            P.op("vector", lambda e: e.scalar_tensor_tensor(out=out, in0=in0, scalar=scalar, in1=in1, op0=op0, op1=op1),
                 reads=R, writes=W)

        def mm(out, lhsT, rhs, start, stop, R, W, inc=True):
            P.op("tensor", lambda e: e.matmul(out, lhsT, rhs, start=start, stop=stop), reads=R, writes=W, inc=inc)

        class Pool_:
            def __init__(self, name, n, shape, dt, dma=False):
                self.t = [sb_at("%s%d" % (name, i), shape, dt) for i in range(n)]
                self.b = [P.buf("%s%d" % (name, i), dma=dma) for i in range(n)]
                self.i = 0

            def get(self):
                i = self.i % len(self.t)
                self.i += 1
                return self.t[i], self.b[i]

        def rstd_from(ps_ap, pb, scale, tmp, out=None):
            lnv, lb_ = tmp.get()
            act(lnv[:], ps_ap, AF.Ln, [pb, cb], [lb_], scale=scale, bias=epsb[:, 0:1])
            r, rb = out if out is not None else tmp.get()
            act(r[:], lnv[:], AF.Exp, [lb_], [rb], scale=-0.5)
            return r, rb

        wtiles = [sb_at("wt%d" % i, [128, 4, 8, 128], BF16) for i in range(NWB)]
        wbufs = [P.buf("wb%d" % i, dma=True) for i in range(NWB)]
        ws.set_bufs(wtiles, wbufs)
        epsb = sb_at("epsb", [128, 1], F32)
        G(lambda e: e.memset(epsb[:], EPS), W=[cb])
        CONST_END = off[0]

        for l in range(L):
            xsrc = xT_in if l == 0 else xs[(l - 1) % 2]
            xdst = outT if l == L - 1 else xs[l % 2]
            VB = l * NVL
            vcol = lambda c: vec[:, VB + c:VB + c + 1]

            if l > 0:
                P.barrier()
            if l == 0 and dbg:
                dwb = P.buf("dwb", dma=True)
                P.dma("gpsimd", dbg_w0, wbf[0:1024, :], writes=[dwb])
                P.dma("gpsimd", dbg_w1, wbf[40 * 128:48 * 128, :], writes=[dwb])
            if l == 0:
                off[0] = CONST_END
                bigf = Pool_("bigf", 3, [128, 8, 512], F32, dma=True)
                stage = Pool_("stg", 2, [128, 8, 512], BF16, dma=True)
                hTp = Pool_("hT", 2, [128, 8, 512], BF16)
                tmp1 = Pool_("tmp", 7, [128, 512], F32, dma=True)
                u_t = sb_at("u", [128, 8, 544], BF16)
                u_b = [P.buf("u%d" % c) for c in range(8)]
                dgs = [sb_at("dg%d" % i, [128, CONV_K, 128], BF16) for i in range(2)]
                dg_bs = [P.buf("dg%d" % i) for i in range(2)]
                Qt = sb_at("Qt", [128, 8, 512], BF16)
                Kt = sb_at("Kt", [128, 8, 512], BF16)
                Ktt = sb_at("Ktt", [128, 8, 512], BF16)
                QK_b = [P.buf("QK%d" % h) for h in range(8)]
                vtk = sb_at("vtk", [128, 4, 1024], BF16)
                vtk_b = P.buf("vtk", dma=True)
                st_f = sb_at("st", [128, 8, 128], F32)
                st_h = sb_at("sth", [128, 8, 128], BF16)
                st_b = [P.buf("st%d" % h) for h in range(8)]
                ebl = sb_at("ebl", [128, 8, 8], F32)
                ebl_b = [P.buf("ebl%d" % h) for h in range(8)]
                kttok = Pool_("ktk", 8, [128, 128], BF16)
                scT = Pool_("scT", 8, [128, 64], BF16)
                Fc_b = P.buf("Fc", dma=True)
                Fcar = sb_at("Fcar", [16, 1], F32)
                Fsp = sb_at("Fsp", [16, 3, 512], BF16)
                Fsp_b = P.buf("Fsp", dma=True)
                vstp = Pool_("vst", 1, [128, 8, 3, 64], BF16, dma=True)
                mean = sb_at("meanA", [128, 512], F32)
                meanb = P.buf("meanA")
                rsA_t = sb_at("rsA", [128, 512], F32)
                rsA_b = P.buf("rsA")
                P1_END = off[0]
            tmp = tmp1
            G(lambda e: e.memset(u_t[:, :, 0:32], 0.0), W=u_b)
            G(lambda e: e.memset(st_f[:], 0.0), W=st_b)
            G(lambda e: e.memset(st_h[:], 0.0), W=st_b)
            G(lambda e: e.memset(Fcar[:], 0.0), W=[Fc_b])
            for t_, b_ in zip(vstp.t, vstp.b):
                G(lambda e, t_=t_: e.memset(t_[:, :, 1, :], 1.0), W=[b_])

            def norm_chunk(ci_):
                tsl_ = slice(ci_ * 512, ci_ * 512 + 512)
                xc, xcb = bigf.get()
                P.dma("sync", xc[:], fm(xsrc)[:, :, tsl_], writes=[xcb])
                hT_, hb_ = hTp.get()
                psS, psSb = ps_rot()
                for k in range(KT):
                    sq, sqb = tmp.get()
                    act(sq[:], xc[:, k, :], AF.Square, [xcb], [sqb])
                    mm(psS[:], ones_f[:], sq[:], k == 0, k == KT - 1, [cb, sqb], [psSb])
                rs, rsb = rstd_from(psS[:], psSb, 1.0 / D, tmp)
                for k in range(KT):
                    stt(hT_[:, k, :], xc[:, k, :], vcol(V_NG + k), rs[:], ALU.mult, ALU.mult, [xcb, rsb, vec_b], [hb_])
                return hT_, hb_

            nxt = norm_chunk(0)
            for ci in range(NCH):
                t0 = ci * 512
                tsl = slice(t0, t0 + 512)
                hT, hb = nxt

                y, yb_ = bigf.get()
                S1, S1b = psum[5], psb[5]
                S2, S2b = psum[6], psb[6]
                def A1(c):
                    wv, wvb = ws.next()
                    wg, wgb = ws.next()
                    pv, pvb = ps_rot()
                    proj(wv, wvb, hT, hb, pv, pvb)
                    pg, pgb = ps_rot()
                    proj(wg, wgb, hT, hb, pg, pgb)
                    sg, sgb = tmp.get()
                    act(sg[:], pg[:], AF.Sigmoid, [pgb], [sgb])
                    tt("vector", u_t[:, c, 32:544], pv[:], sg[:], ALU.mult, [pvb, sgb], [u_b[c]])
                    dg, dg_b = dgs[c % 2], dg_bs[c % 2]
                    P.op("gpsimd", lambda e, c=c, l=l, dg=dg: e.tensor_tensor(
                        out=dg[:], in0=id_b[:].unsqueeze(1).to_broadcast([128, CONV_K, 128]),
                        in1=cw_b[:, l, c, :].unsqueeze(2).to_broadcast([128, CONV_K, 128]), op=ALU.mult),
                        reads=[cb], writes=[dg_b])

                def A2(c):
                    dg, dg_b = dgs[c % 2], dg_bs[c % 2]
                    py, pyb = ps_rot()
                    for j in range(CONV_K):
                        mm(py[:], dg[:, j, :], u_t[:, c, 2 + j:2 + j + 512], j == 0, j == CONV_K - 1,
                           [dg_b, u_b[c]], [pyb], inc=(j == CONV_K - 1))
                    act(y[:, c, :], py[:], AF.Identity, [pyb, vec_b], [yb_], bias=vcol(V_CB + c))
                    ysq, ysqb = tmp.get()
                    act(ysq[:], py[:], AF.Square, [pyb, vec_b], [ysqb], bias=vcol(V_CB + c))
                    mm(S1[:], ones_f[:], y[:, c, :], c == 0, c == 7, [cb, yb_], [S1b])
                    mm(S2[:], ones_f[:], ysq[:], c == 0, c == 7, [cb, ysqb], [S2b])

                for c in range(8):
                    A1(c)
                    if c > 0:
                        A2(c - 1)
                A2(7)
                P.op("gpsimd", lambda e: e.tensor_copy(out=u_t[:, :, 2:32], in_=u_t[:, :, 514:544]), reads=u_b, writes=u_b)
                act(mean[:], S1[:], AF.Identity, [S1b], [meanb], scale=1.0 / D)
                msq, msqb = tmp.get()
                tt("gpsimd", msq[:], mean[:], mean[:], ALU.mult, [meanb], [msqb])
                var, varb = tmp.get()
                stt(var[:], S2[:], 1.0 / D, msq[:], ALU.mult, ALU.subtract, [S2b, msqb], [varb])
                rsA, rsAb = rstd_from(var[:], varb, 1.0, tmp, out=(rsA_t, rsA_b))
                ya, yab = stage.get()
                def LN_c(c):
                        t1, t1b = tmp.get()
                        tt("vector", t1[:], y[:, c, :], mean[:], ALU.subtract, [yb_, meanb], [t1b])
                        t2, t2b = tmp.get()
                        tt("gpsimd", t2[:], t1[:], rsA[:], ALU.mult, [t1b, rsAb], [t2b])
                        t3, t3b = tmp.get()
                        ts("vector", t3[:], t2[:], vcol(V_LG + c), ALU.mult, [t2b, vec_b], [t3b], s2=vcol(V_LB + c), op1=ALU.add)
                        s3, s3b = tmp.get()
                        act(s3[:], t3[:], AF.Sigmoid, [t3b], [s3b])
                        t4, t4b = tmp.get()
                        tt("gpsimd", t4[:], t3[:], s3[:], ALU.mult, [t3b, s3b], [t4b])
                        wgt, wgtb = ws.next()
                        pgt, pgtb = ps_rot()
                        proj(wgt, wgtb, hT, hb, pgt, pgtb)
                        sgt, sgtb = tmp.get()
                        act(sgt[:], pgt[:], AF.Sigmoid, [pgtb], [sgtb])
                        t5, t5b = tmp.get()
                        tt("vector", t5[:], pgt[:], sgt[:], ALU.mult, [pgtb, sgtb], [t5b])
                        tt("vector", ya[:, c, :], t4[:], t5[:], ALU.mult, [t4b, t5b], [yab])

                def BH(h):
                        wq, wqb = ws.next()
                        pq, pqb = ps_rot()
                        proj(wq, wqb, hT, hb, pq, pqb)
                        wf, wfb = ws.next()
                        pf, pfb = ps_rot()
                        proj(wf, wfb, hT, hb, pf, pfb)
                        s_q, s_qb = tmp.get()
                        act(s_q[:], pq[:], AF.Sigmoid, [pqb], [s_qb])
                        qf, qfb = tmp.get()
                        tt("vector", qf[:], pq[:], s_q[:], ALU.mult, [pqb, s_qb], [qfb])
                        s_f, s_fb = tmp.get()
                        act(s_f[:], pf[:], AF.Sigmoid, [pfb], [s_fb])
                        kk, kkb = tmp.get()
                        ts("vector", kk[:], s_f[:], lbt[:, l, 2, h:h + 1], ALU.mult, [s_fb, cb], [kkb],
                           s2=lbt[:, l, 1, h:h + 1], op1=ALU.add)
                        d0, d0b = tmp.get()
                        stt(d0[:], kk[:], 1.0, rmask[:], ALU.subtract, ALU.mult, [kkb, cb], [d0b])
                        d1, d1b = tmp.get()
                        P.op("gpsimd", lambda e, d1=d1: e.memset(d1[:], 0.0), writes=[d1b])
                        P.op("gpsimd", lambda e, d1=d1, kk=kk: e.tensor_scalar(
                            out=d1[:].rearrange("p (c t) -> p c t", t=64)[:, :, 0:1],
                            in0=kk[:].rearrange("p (c t) -> p c t", t=64)[:, :, 0:1],
                            scalar1=-1.0, scalar2=1.0, op0=ALU.mult, op1=ALU.add), reads=[kkb], writes=[d1b])
                        eb, ebb = tmp.get()
                        P.op("vector", lambda e, eb=eb, d0=d0, d1=d1: e.tensor_tensor_scan(
                            out=eb[:], data0=d0[:], data1=d1[:], initial=0.0, op0=ALU.mult, op1=ALU.add),
                            reads=[d0b, d1b], writes=[ebb])
                        tt("vector", Qt[:, h, :], qf[:], eb[:], ALU.mult, [qfb, ebb], [QK_b[h]])
                        enb, enbb = tmp.get()
                        P.op("vector", lambda e, eb=eb, enb=enb: e.reciprocal(out=enb[:], in_=eb[:]),
                             reads=[ebb], writes=[enbb])
                        tt("gpsimd", Kt[:, h, :], kk[:], enb[:], ALU.mult, [kkb, enbb], [QK_b[h]])
                        ebv = eb[:].rearrange("p (c t) -> p c t", t=64)[:, :, 63:64]
                        P.op("vector", lambda e, h=h, ebv=ebv: e.tensor_tensor(
                            out=Ktt[:, h, :].rearrange("p (c t) -> p c t", t=64),
                            in0=Kt[:, h, :].rearrange("p (c t) -> p c t", t=64),
                            in1=ebv.to_broadcast([128, 8, 64]), op=ALU.mult),
                            reads=[QK_b[h], ebb], writes=[QK_b[h]])
                        P.op("gpsimd", lambda e, h=h, ebv=ebv: e.tensor_copy(out=ebl[:, h, :].unsqueeze(2), in_=ebv),
                             reads=[ebb], writes=[ebl_b[h]])

                for i_ in range(8):
                    LN_c(i_)
                    BH(i_)
                if dbg:
                    yaf, yafb = bigf.get()
                    VV(lambda e: e.tensor_copy(out=yaf[:], in_=ya[:]), R=[yab], W=[yafb])
                    P.dma("sync", fm(dbg_d["ya"])[:, :, tsl], yaf[:], reads=[yafb], writes=[P.buf("dd")])
                if ci + 1 < NCH:
                    nxt = norm_chunk(ci + 1)
                mAB, mABb = bigf.get()
                for d in range(8):
                    wb_, wbb = ws.next()
                    wm, wmb = ws.next()
                    pd, pdb = ps_rot()
                    proj(wb_, wbb, ya, yab, pd, pdb)
                    pm, pmb = ps_rot()
                    proj(wm, wmb, hT, hb, pm, pmb)
                    gm, gmb = tmp.get()
                    act(gm[:], pm[:], AF.Sigmoid, [pmb], [gmb])
                    tt("vector", mAB[:, d, :], pd[:], gm[:], ALU.mult, [pdb, gmb], [mABb])

                o_sb, ob = bigf.get()
                for half in range(2):
                    wgp, wgpb = ws.next_group()
                    for tq in range(4):
                        pv, pvb = ps_rot()
                        for k in range(KT):
                            mm(pv[:], hT[:, k, tq * 128:(tq + 1) * 128], wgp[:, :, k, :], k == 0, k == KT - 1,
                               [hb, wgpb], [pvb], inc=(k == KT - 1))
                        act(vtk[:, tq, half * 512:(half + 1) * 512], pv[:], AF.Identity, [pvb], [vtk_b])
                for tq in range(4):
                    c0 = tq * 128
                    kt_l, sT_l = [], []
                    for h in range(8):
                        P.op("tensor", lambda e, h=h, c0=c0: e.transpose(pstr[:], Ktt[:, h, c0:c0 + 128], id_b[:]),
                             reads=[QK_b[h], cb], writes=[pstr_b])
                        ktk, ktkb = kttok.get()
                        VV(lambda e, ktk=ktk: e.tensor_copy(out=ktk[:], in_=pstr[:]), R=[pstr_b], W=[ktkb])
                        pss, pssb = ps_rot()
                        for j in range(2):
                            cj = c0 + j * 64
                            mm(pss[j * 64:(j + 1) * 64, 0:64], Kt[:, h, cj:cj + 64], Qt[:, h, cj:cj + 64], True, True,
                               [QK_b[h]], [pssb], inc=(j == 1))
                        sT, sTb = scT.get()
                        tt("vector", sT[:], pss[:, 0:64], maskB[:], ALU.mult, [pssb, cb], [sTb])
                        kt_l.append((ktk, ktkb))
                        sT_l.append((sT, sTb))
                    for j in range(2):
                        cj = c0 + j * 64
                        r = slice(j * 64, j * 64 + 64)
                        for h in range(8):
                            ktk, ktkb = kt_l[h]
                            sT, sTb = sT_l[h]
                            po, pob = ps_rot()
                            mm(po[:, 0:64], vtk[r, tq, h * 128:(h + 1) * 128], sT[r, :], True, False,
                               [vtk_b, sTb], [pob], inc=False)
                            mm(po[:, 0:64], st_h[:, h, :], Qt[:, h, cj:cj + 64], False, True,
                               [st_b[h], QK_b[h]], [pob], inc=True)
                            pu, pub = ps_rot()
                            mm(pu[:, 0:128], ktk[r, :], vtk[r, tq, h * 128:(h + 1) * 128], True, True,
                               [ktkb, vtk_b], [pub])
                            stt(st_f[:, h, :], st_f[:, h, :], ebl[:, h, tq * 2 + j:tq * 2 + j + 1], pu[:, 0:128],
                                ALU.mult, ALU.add, [st_b[h], ebl_b[h], pub], [st_b[h]])
                            P.op("gpsimd", lambda e, h=h: e.tensor_copy(out=st_h[:, h, :], in_=st_f[:, h, :]),
                                 reads=[st_b[h]], writes=[st_b[h]])
                            act(o_sb[:, h, cj:cj + 64], po[:, 0:64], AF.Identity, [pob], [ob])
                yB, yBb = stage.get()
                rsBall, rsBallb = bigf.get()

                def B1(h):
                    osq, osqb = tmp.get()
                    act(osq[:], o_sb[:, h, :], AF.Square, [ob], [osqb])
                    return osq, osqb

                def B2(h, osq, osqb):
                    pn, pnb = ps_rot()
                    mm(pn[:], ones_f[:], osq[:], True, True, [cb, osqb], [pnb])
                    act(rsBall[:, h, :], pn[:], AF.Identity, [pnb, cb], [rsBallb], scale=1.0 / 128, bias=epsb[:, 0:1])

                pb1 = None
                for h in range(8):
                    cur = B1(h)
                    if pb1 is not None:
                        B2(h - 1, *pb1)
                    pb1 = cur
                B2(7, *pb1)
                rsv = rsBall[:].rearrange("p k t -> p (k t)")
                act(rsv, rsv, AF.Ln, [rsBallb], [rsBallb])
                act(rsv, rsv, AF.Exp, [rsBallb], [rsBallb], scale=-0.5)
                for h in range(8):
                    wgt, wgtb = ws.next()
                    pgt, pgtb = ps_rot()
                    proj(wgt, wgtb, hT, hb, pgt, pgtb)
                    sgt, sgtb = tmp.get()
                    act(sgt[:], pgt[:], AF.Sigmoid, [pgtb], [sgtb])
                    t5, t5b = tmp.get()
                    tt("vector", t5[:], pgt[:], sgt[:], ALU.mult, [pgtb, sgtb], [t5b])
                    t1, t1b = tmp.get()
                    stt(t1[:], o_sb[:, h, :], vcol(V_HG + h), rsBall[:, h, :], ALU.mult, ALU.mult, [ob, rsBallb, vec_b], [t1b])
                    tt("gpsimd", yB[:, h, :], t1[:], t5[:], ALU.mult, [t1b, t5b], [yBb])
                if dbg:
                    yaf, yafb = bigf.get()
                    VV(lambda e: e.tensor_copy(out=yaf[:], in_=yB[:]), R=[yBb], W=[yafb])
                    P.dma("sync", fm(dbg_d["yb"])[:, :, tsl], yaf[:], reads=[yafb], writes=[P.buf("dd")])
                for d in range(8):
                    wb_, wbb = ws.next()
                    wm, wmb = ws.next()
                    pd, pdb = ps_rot()
                    proj(wb_, wbb, yB, yBb, pd, pdb)
                    pm, pmb = ps_rot()
                    proj(wm, wmb, hT, hb, pm, pmb)
                    gm, gmb = tmp.get()
                    act(gm[:], pm[:], AF.Sigmoid, [pmb], [gmb])
                    t6, t6b = tmp.get()
                    tt("vector", t6[:], pd[:], gm[:], ALU.mult, [pdb, gmb], [t6b])
                    tt("gpsimd", mAB[:, d, :], mAB[:, d, :], t6[:], ALU.add, [mABb, t6b], [mABb])
                P.dma("gpsimd", fm(mAB_d)[:, :, tsl], mAB[:], reads=[mABb], writes=[P.buf("mABd%d" % ci)])

                for which, dst in ((0, qn_d), (1, kn_d)):
                    qn, qnb = stage.get()
                    def C1(pr):
                        wq, wqb = ws.next()
                        pq, pqb = ps_rot()
                        proj(wq, wqb, hT, hb, pq, pqb)
                        qsq, qsqb = tmp.get()
                        act(qsq[:], pq[:], AF.Square, [pqb], [qsqb])
                        return pq, pqb, qsq, qsqb

                    def C2(pr, pq, pqb, qsq, qsqb, which=which, qn=qn, qnb=qnb):
                        pn, pnb = ps_rot()
                        mm(pn[:], blk_f[:], qsq[:], True, True, [cb, qsqb], [pnb])
                        rsC, rsCb = rstd_from(pn[:], pnb, 1.0 / 64, tmp)
                        stt(qn[:, pr, :], pq[:], vcol(V_QG if which == 0 else V_KG), rsC[:], ALU.mult, ALU.mult,
                            [pqb, rsCb, vec_b], [qnb])

                    pc1 = None
                    for pr in range(8):
                        cur = C1(pr)
                        if pc1 is not None:
                            C2(pr - 1, *pc1)
                        pc1 = cur
                    C2(7, *pc1)
                    P.dma("gpsimd", fm(dst)[:, :, tsl], qn[:], reads=[qnb], writes=[P.buf("qkd")])
                wgps = [ws.next_group(), ws.next_group(ahead=1)]
                for tq in range(4):
                    vs_, vsb = vstp.get()
                    for half in range(2):
                        wgp, wgpb = wgps[half]
                        pv, pvb = ps_rot()
                        for k in range(KT):
                            mm(pv[:], hT[:, k, tq * 128:(tq + 1) * 128], wgp[:, :, k, :], k == 0, k == KT - 1,
                               [hb, wgpb], [pvb], inc=(k == KT - 1))
                        act(vs_[:, half * 4:(half + 1) * 4, 0:3:2, :], pv[:].rearrange("p (a b c) -> p a b c", b=2, c=64),
                            AF.Identity, [pvb], [vsb])
                    P.dma("gpsimd", v_d[t0 + tq * 128:t0 + (tq + 1) * 128, :], vs_[:].rearrange("p a b c -> p (a b c)"),
                          reads=[vsb], writes=[P.buf("vd")])
                sC, sCb = stage.get()
                for pr in range(8):
                    wgt, wgtb = ws.next()
                    pgt, pgtb = ps_rot()
                    proj(wgt, wgtb, hT, hb, pgt, pgtb)
                    sgt, sgtb = tmp.get()
                    act(sgt[:], pgt[:], AF.Sigmoid, [pgtb], [sgtb])
                    tt("vector", sC[:, pr, :], pgt[:], sgt[:], ALU.mult, [pgtb, sgtb], [sCb])
                P.dma("gpsimd", fm(sc_d)[:, :, tsl], sC[:], reads=[sCb], writes=[P.buf("scd")])
                gC, gCb = bigf.get()
                for d in range(8):
                    wm, wmb = ws.next()
                    pm, pmb = ps_rot()
                    proj(wm, wmb, hT, hb, pm, pmb)
                    act(gC[:, d, :], pm[:], AF.Sigmoid, [pmb], [gCb])
                P.dma("gpsimd", fm(gC_d)[:, :, tsl], gC[:], reads=[gCb], writes=[P.buf("gCd")])
                pF, pFb = ps_rot()
                for k in range(KT):
                    mm(pF[0:16, :], wcf_b[:, l, k * 16:(k + 1) * 16], hT[:, k, :], k == 0, k == KT - 1,
                       [wcf_buf, hb], [pFb], inc=(k == KT - 1))
                fa, fab = tmp.get()
                fbx, fbxb = tmp.get()
                Fc_t, FcB = tmp.get()
                Fc = Fc_t[0:16, :]
                act(fa[0:16, :], pF[0:16, :], AF.Sigmoid, [pFb, vec_b], [fab], bias=vec[0:16, VB + V_FB:VB + V_FB + 1])
                act(fbx[0:16, :], fa[0:16, :], AF.Ln, [fab], [fbxb])
                P.op("vector", lambda e, fbx=fbx, Fc=Fc: e.tensor_tensor_scan(out=Fc, data0=onesrow[0:16, :], data1=fbx[0:16, :],
                                                              initial=Fcar[:, 0:1], op0=ALU.mult, op1=ALU.add),
                     reads=[cb, fbxb, Fc_b], writes=[FcB])
                VV(lambda e, Fc=Fc: e.tensor_copy(out=Fcar[:], in_=Fc[:, 511:512]), R=[FcB], W=[Fc_b])
                P.dma("gpsimd", F_d[:, tsl], Fc, reads=[FcB], writes=[P.buf("Fd")])
                ts("vector", fa[0:16, :], Fc, 8.0, ALU.mult, [FcB], [fab])
                VV(lambda e, fa=fa: e.tensor_copy(out=Fsp[:, 0, :], in_=fa[0:16, :]), R=[fab], W=[Fsp_b])
                tt("vector", fbx[0:16, :], fa[0:16, :], Fsp[:, 0, :], ALU.subtract, [fab, Fsp_b], [fbxb])
                VV(lambda e, fbx=fbx: e.tensor_copy(out=Fsp[:, 1, :], in_=fbx[0:16, :]), R=[fbxb], W=[Fsp_b])
                tt("vector", fa[0:16, :], fbx[0:16, :], Fsp[:, 1, :], ALU.subtract, [fbxb, Fsp_b], [fab])
                VV(lambda e, fa=fa: e.tensor_copy(out=Fsp[:, 2, :], in_=fa[0:16, :]), R=[fab], W=[Fsp_b])
                P.dma("gpsimd", Fs_d[:, :, tsl], Fsp[:], reads=[Fsp_b], writes=[P.buf("Fsd")])

            P.barrier()
            if l == 0:
                off[0] = CONST_END
                Kp = Pool_("Kp", 2, [128, T], BF16, dma=True)
                Vp = Pool_("Vp", 2, [128, NTT, 192], BF16, dma=True)
                negF = sb_at("negF", [128, NTT, 16], F32)
                negF_b = P.buf("negF")
                FT = sb_at("FT", [16, 512], F32)
                FT_b = P.buf("FT", dma=True)
                qzp = [Pool_("qz%d_" % i, 2, [128, 512], BF16, dma=True) for i in range(2)]
                fbp = Pool_("fbp", 4, [128, 512], BF16, dma=True)
                scp = Pool_("scp", 2, [128, 512], BF16, dma=True)
                ptp = Pool_("ptp", 4, [128, 512], BF16)
                ycg = Pool_("ycg", 2, [128, 512], BF16, dma=True)
                tmp2 = Pool_("tm2", 8, [128, 512], F32, dma=True)
                ycTs = [sb_at("ycT%d" % i, [128, 8, 512], BF16) for i in range(2)]
                ycbs = [P.buf("ycT%d" % i, dma=True) for i in range(2)]
                mixb = sb_at("mixb", [128, 8, 512], BF16)
                mixbb = P.buf("mixb")
                P2_END = off[0]
            tmp = tmp2
            for t_, b_ in zip(fbp.t + qzp[0].t + qzp[1].t, fbp.b + qzp[0].b + qzp[1].b):
                G(lambda e, t_=t_: e.memset(t_[:], 0.0), W=[b_])
            for ci in range(NCH):
                tsl = slice(ci * 512, ci * 512 + 512)
                P.dma("sync", FT[:], F_d[:, tsl], writes=[FT_b])
                for a_ in range(4):
                    tq = ci * 4 + a_
                    pT, pTb = ps_rot(3)
                    P.op("tensor", lambda e, a_=a_, pT=pT: e.transpose(pT[:, 0:16], FT[:, a_ * 128:(a_ + 1) * 128], id_f[0:16, 0:16]),
                         reads=[FT_b, cb], writes=[pTb])
                    ts("vector", negF[:, tq, :], pT[:, 0:16], -1.0, ALU.mult, [pTb], [negF_b])

            Fs_v = Fs_d.rearrange("(pr par) r t -> par r pr t", par=2)
            for pr in range(8):
                Kt_, Ktb = Kp.get()
                P.dma("sync", Kt_[:], kn_d[pr * 128:(pr + 1) * 128, :], writes=[Ktb])
                Vt_, Vtb = Vp.get()
                P.dma("sync", Vt_[:], v_d[:, pr * 192:(pr + 1) * 192].rearrange("(a p) n -> p a n", p=128), writes=[Vtb])
                for ci in range(NCH):
                    t0 = ci * 512
                    tsl = slice(t0, t0 + 512)
                    qz, fbt = [], []
                    for par in range(2):
                        r = slice(par * 64, par * 64 + 64)
                        qt_, qtb = qzp[par].get()
                        P.dma("sync", qt_[r, :], fm(qn_d)[r, pr, tsl], writes=[qtb])
                        qz.append((qt_, qtb))
                        fb_, fbb = fbp.get()
                        P.dma("sync", fb_[0:3, :], Fs_v[par, :, pr, tsl], writes=[fbb])
                        fbt.append((fb_, fbb))
                    sct, sctb = scp.get()
                    P.dma("sync", sct[:], fm(sc_d)[:, pr, tsl], writes=[sctb])
                    par_ = (pr * NCH + ci) % 2
                    Oacc = [(psum[3 + 2 * par_], psb[3 + 2 * par_]), (psum[4 + 2 * par_], psb[4 + 2 * par_])]
                    nkt = 4 * (ci + 1)
                    blocks = [(kt, par) for kt in range(nkt) for par in range(2)]

                    def qk(i):
                        kt, par = blocks[i]
                        q_lo = max(0, kt - 4 * ci) * 128
                        pS, pSb = ps_rot(3)
                        mm(pS[:, q_lo:512], Kt_[:, kt * 128:(kt + 1) * 128], qz[par][0][:, q_lo:512], True, False,
                           [Ktb, qz[par][1]], [pSb], inc=False)
                        diag = (kt - 4 * ci) >= 0
                        mm(pS[:, q_lo:512], onesK[:], fbt[par][0][:, q_lo:512], False, not diag, [cb, fbt[par][1]], [pSb],
                           inc=not diag)
                        if diag:
                            mm(pS[:, q_lo:q_lo + 128], id_b[:], maskN[:], False, True, [cb], [pSb])
                        return pS, pSb

                    def pv(i, pS, pSb):
                        kt, par = blocks[i]
                        a = kt - 4 * ci
                        q_lo = max(0, a) * 128
                        h = 2 * pr + par
                        pt, ptb = ptp.get()
                        act(pt[:, q_lo:512], pS[:, q_lo:512], AF.Exp, [pSb, negF_b], [ptb], scale=0.125,
                            bias=negF[:, kt, h:h + 1])
                        O, Ob = Oacc[par]
                        mm(O[:, q_lo:512], Vt_[:, kt, par * 64:par * 64 + 128], pt[:, q_lo:512], kt == 0, kt == nkt - 1,
                           [Vtb, ptb], [Ob])

                    DEPTH_ = 2
                    pend = [qk(i) for i in range(min(DEPTH_, len(blocks)))]
                    for i in range(len(blocks)):
                        if i + DEPTH_ < len(blocks):
                            pend.append(qk(i + DEPTH_))
                        pS, pSb = pend.pop(0)
                        pv(i, pS, pSb)
                    (OA, OAb), (OB, OBb) = Oacc
                    R_, Rb = tmp.get()
                    VV(lambda e, R_=R_, OB=OB: e.reciprocal(out=R_[0:64, :], in_=OB[0:64, :]), R=[OBb], W=[Rb])
                    VV(lambda e, R_=R_, OA=OA: e.reciprocal(out=R_[64:128, :], in_=OA[64:128, :]), R=[OAb], W=[Rb])
                    Rs, Rsb = tmp.get()
                    P.dma("gpsimd", Rs[0:64, :], R_[64:128, :], reads=[Rb], writes=[Rsb])
                    P.dma("gpsimd", Rs[64:128, :], R_[0:64, :], reads=[Rb], writes=[Rsb])
                    t1, t1b = tmp.get()
                    tt("vector", t1[0:64, :], OA[0:64, :], Rs[0:64, :], ALU.mult, [OAb, Rsb], [t1b])
                    tt("vector", t1[64:128, :], OB[64:128, :], Rs[64:128, :], ALU.mult, [OBb, Rsb], [t1b])
                    if dbg:
                        P.dma("sync", fm(dbg_d["yc"])[:, pr, tsl], t1[:], reads=[t1b], writes=[P.buf("dd")])
                    yg, ygb = ycg.get()
                    tt("gpsimd", yg[:], t1[:], sct[:], ALU.mult, [t1b, sctb], [ygb])
                    P.dma("gpsimd", yc_d[pr * 128:(pr + 1) * 128, tsl], yg[:], reads=[ygb], writes=[P.buf("ycd")])
            P.barrier()
            P.dma("sync", ycTs[0][:], fm(yc_d)[:, :, 0:512], writes=[ycbs[0]])
            for ci in range(NCH):
                t0 = ci * 512
                tsl = slice(t0, t0 + 512)
                ycT, ycb = ycTs[ci % 2], ycbs[ci % 2]
                if ci + 1 < NCH:
                    P.dma("sync", ycTs[(ci + 1) % 2][:], fm(yc_d)[:, :, t0 + 512:t0 + 1024], writes=[ycbs[(ci + 1) % 2]])

                def ld_d(d, tsl=tsl):
                    gt, gtb = tmp.get()
                    P.dma("sync", gt[:], fm(gC_d)[:, d, tsl], writes=[gtb])
                    mt, mtb = tmp.get()
                    P.dma("sync", mt[:], fm(mAB_d)[:, d, tsl], writes=[mtb])
                    return gt, gtb, mt, mtb

                def ld_e(e_, tsl=tsl):
                    xt_, xtb = tmp.get()
                    P.dma("sync", xt_[:], fm(xsrc)[:, e_, tsl], writes=[xtb])
                    return xt_, xtb

                cur = ld_d(0)
                for d in range(8):
                    nx = ld_d(d + 1) if d + 1 < 8 else ld_e(0)
                    gt, gtb, mt, mtb = cur
                    wb_, wbb = ws.next()
                    pd, pdb = ps_rot(3)
                    proj(wb_, wbb, ycT, ycb, pd, pdb)
                    t2, t2b = tmp.get()
                    tt("vector", t2[:], pd[:], gt[:], ALU.mult, [pdb, gtb], [t2b])
                    tt("gpsimd", mixb[:, d, :], t2[:], mt[:], ALU.add, [t2b, mtb], [mixbb])
                    cur = nx
                for e_ in range(8):
                    nx = ld_e(e_ + 1) if e_ + 1 < 8 else None
                    xt_, xtb = cur
                    wo, wob = ws.next()
                    pe, peb = ps_rot(3)
                    proj(wo, wob, mixb, mixbb, pe, peb)
                    xo, xob = tmp.get()
                    tt("vector", xo[:], pe[:], xt_[:], ALU.add, [peb, xtb], [xob])
                    P.dma("gpsimd", fm(xdst)[:, e_, tsl], xo[:], reads=[xob], writes=[P.buf("xd")])
                    cur = nx

        P.barrier()
        P.emit()
    return nc


def _tile(w_cols):
    return w_cols.reshape(8, 128, 128).transpose(1, 0, 2)


def prep_weights(L, norm_g, w_in, conv_w, conv_b, conv_ln_g, conv_ln_b, hgrn_lb_logits, hgrn_norm_g,
                 fox_f_bias, fox_qn_g, fox_kn_g, w_branch, w_out):
    off = {"a_val": 0, "a_glu": 1024, "a_gate": 2048, "b_q": 3072, "b_f": 4096, "b_i": 5120, "b_gate": 6144,
           "c_q": 7168, "c_k": 8192, "c_v": 9216, "c_gate": 10240, "c_f": 11264, "m_a": 11280, "m_b": 12304,
           "m_c": 13328}
    wst = np.empty((L, NT, 128, 8, 128), np.float32)
    wcf = np.empty((L, 128, 8, 16), np.float32)
    vec = np.zeros((128, L, NVL), np.float32)
    pk = lambda v: v.reshape(8, 128).T
    for l in range(L):
        W = w_in[l]
        it = lambda name, i: _tile(W[:, off[name] + i * 128: off[name] + (i + 1) * 128])
        bt = lambda br, i: _tile(w_branch[l, br][:, i * 128:(i + 1) * 128])
        tiles = []
        for c in range(8):
            tiles += [it("a_val", c), it("a_glu", c)]
        for i in range(8):
            tiles += [it("a_gate", i), it("b_q", i), it("b_f", i)]
        for d in range(8):
            tiles += [bt(0, d), it("m_a", d)]
        tiles += [it("b_i", i) for i in range(8)]
        tiles += [it("b_gate", h) for h in range(8)]
        for d in range(8):
            tiles += [bt(1, d), it("m_b", d)]
        tiles += [it("c_q", i) for i in range(8)]
        tiles += [it("c_k", i) for i in range(8)]
        tiles += [it("c_v", i) for i in range(8)]
        tiles += [it("c_gate", i) for i in range(8)]
        tiles += [it("m_c", i) for i in range(8)]
        tiles += [bt(2, d) for d in range(8)]
        tiles += [_tile(w_out[l][:, e * 128:(e + 1) * 128]) for e in range(8)]
        assert len(tiles) == NT
        wst[l] = np.stack(tiles)
        wcf[l] = W[:, off["c_f"]:off["c_f"] + 16].reshape(8, 128, 16).transpose(1, 0, 2)
        vec[:, l, V_NG:V_NG + 8] = pk(norm_g[l])
        vec[:, l, V_CB:V_CB + 8] = pk(conv_b[l])
        vec[:, l, V_LG:V_LG + 8] = pk(conv_ln_g[l])
        vec[:, l, V_LB:V_LB + 8] = pk(conv_ln_b[l])
        vec[:, l, V_HG:V_HG + 8] = pk(hgrn_norm_g[l])
        vec[:, l, V_LBL:V_LBL + 8] = pk(hgrn_lb_logits[l])
        vec[:, l, V_CW:V_CW + 248] = conv_w[l].reshape(CONV_K, 8, 128).transpose(2, 1, 0).reshape(128, 248)
        vec[:, l, V_QG] = np.tile(fox_qn_g[l], 2)
        vec[:, l, V_KG] = np.tile(fox_kn_g[l], 2)
        vec[:16, l, V_FB] = fox_f_bias[l]
    return (np.ascontiguousarray(wst.reshape(L * NT * 128, 1024)),
            np.ascontiguousarray(wcf.reshape(L * 128, 128)),
            np.ascontiguousarray(vec.reshape(128, L * NVL)))


_NC_CACHE = {}


def run(x, params, L, dbg=False, n_cores=None):
    B, T, _ = x.shape
    n_cores = n_cores or B
    key = (T, L, dbg)
    if key not in _NC_CACHE:
        _NC_CACHE[key] = build(T, L, dbg)
    nc = _NC_CACHE[key]
    wst, wcf, vec = prep_weights(L, **params)
    in_maps = [{"xT": np.ascontiguousarray(x[b].T), "wst": wst, "wcf": wcf, "vec": vec} for b in range(n_cores)]
    res = run_bass_kernel_spmd(nc, in_maps, core_ids=list(range(n_cores)))
    return res.results


def kernel(x, norm_g, w_in, conv_w, conv_b, conv_ln_g, conv_ln_b, hgrn_lb_logits, hgrn_norm_g,
           fox_f_bias, fox_qn_g, fox_kn_g, w_branch, w_out):
    x = np.asarray(x, np.float32)
    params = dict(norm_g=norm_g, w_in=w_in, conv_w=conv_w, conv_b=conv_b, conv_ln_g=conv_ln_g,
                  conv_ln_b=conv_ln_b, hgrn_lb_logits=hgrn_lb_logits, hgrn_norm_g=hgrn_norm_g,
                  fox_f_bias=fox_f_bias, fox_qn_g=fox_qn_g, fox_kn_g=fox_kn_g, w_branch=w_branch, w_out=w_out)
    params = {k: np.asarray(v, np.float32) for k, v in params.items()}
    L = params["w_in"].shape[0]
    results = run(x, params, L)
    out = np.stack([np.ascontiguousarray(r["outT"].T) for r in results], axis=0)
    return out.astype(np.float32)
```

```python
import numpy as np
from contextlib import ExitStack
import concourse.bass as bass
import concourse.mybir as mybir
from concourse.bass_utils import run_bass_kernel_spmd

F32 = mybir.dt.float32
BF16 = mybir.dt.bfloat16
AF = mybir.ActivationFunctionType
ALU = mybir.AluOpType

D = 1024
KT = 8
CONV_K = 31
EPS = 1e-6
NT = 144
NG = NT // 4
P1_TILES = 128
V_NG, V_CB, V_LG, V_LB, V_HG, V_LBL, V_CW, V_QG, V_KG, V_FB, NVL = 0, 8, 16, 24, 32, 40, 48, 296, 297, 298, 300


class SemC:
    def __init__(self, h, is_dma=False):
        self.h = h
        self.v = 0
        self.is_dma = is_dma


class Buf:
    def __init__(self, name, dsem=None):
        self.name = name
        self.w = None
        self.r = {}
        self.dsem = dsem


class Eng:
    def __init__(self, name, sem):
        self.name = name
        self.done = SemC(sem)
        self.ops = []
        self.waited = {}


class Prog:
    def __init__(self, nc, stack):
        self.nc = nc
        self.stack = stack
        self.engs = {}
        for n in ("tensor", "vector", "scalar", "gpsimd", "sync"):
            self.engs[n] = Eng(n, stack.enter_context(nc.semaphore("done_" + n)))
        self.dsems = []
        self.nops = 0

    def buf(self, name, dma=False):
        dsem = None
        if dma:
            dsem = SemC(self.stack.enter_context(self.nc.semaphore("d_" + name)), True)
            self.dsems.append(dsem)
        return Buf(name, dsem)

    def _wait(self, e, deps):
        for sc, val in deps:
            if sc.is_dma:
                val = sc.v
            if e.waited.get(sc, 0) < val:
                e.waited[sc] = val
                h = sc.h
                e.ops.append(lambda eng, h=h, val=val: eng.wait_ge(h, val))

    def _deps(self, e, reads, writes, skip_self=False):
        deps = []
        for b in reads:
            if b.w is not None:
                deps.append(b.w)
        for b in writes:
            if b.w is not None:
                deps.append(b.w)
            deps.extend(b.r.items())
        if skip_self:
            deps = [d for d in deps if d[0] is not e.done]
        return deps

    def op(self, en, fn, reads=(), writes=(), inc=True):
        e = self.engs[en]
        self.nops += 1
        self._wait(e, self._deps(e, reads, writes, skip_self=(en == "tensor")))
        if inc:
            e.done.v += 1
            tokv = e.done.v
            h = e.done.h
            e.ops.append(lambda eng, fn=fn, h=h: fn(eng).then_inc(h, 1))
        else:
            tokv = e.done.v + 1
            e.ops.append(lambda eng, fn=fn: fn(eng))
        for b in reads:
            b.r[e.done] = tokv
        for b in writes:
            b.w = (e.done, tokv)
            b.r = {}

    def dma(self, en, out_ap, in_ap, reads=(), writes=(), sem=None, **kw):
        e = self.engs[en]
        self.nops += 1
        if sem is None:
            for b in list(writes) + list(reads):
                if b.dsem is not None:
                    sem = b.dsem
                    break
        assert sem is not None
        self._wait(e, self._deps(e, reads, writes))
        sem.v += 16
        h = sem.h
        e.ops.append(lambda eng, h=h, o=out_ap, i=in_ap, kw=kw:
                     eng.dma_start(out=o, in_=i, **kw).then_inc(h, 16))
        for b in reads:
            b.r[sem] = sem.v
        for b in writes:
            b.w = (sem, sem.v)
            b.r = {}

    def wait_all(self, en, bufs):
        e = self.engs[en]
        self._wait(e, [b.w for b in bufs if b.w is not None])

    def barrier(self):
        toks = [(e.done, e.done.v) for e in self.engs.values() if e.done.v > 0]
        toks += [(s, s.v) for s in self.dsems if s.v > 0]
        for e in self.engs.values():
            self._wait(e, [t for t in toks if t[0] is not e.done])

    def emit(self):
        with self.nc.Block() as block:
            for n, e in self.engs.items():
                if not e.ops:
                    continue

                def body(eng, ops=e.ops):
                    for f in ops:
                        f(eng)
                getattr(block, n)(body)


def build(T, L, dbg=False):
    NCH = T // 512
    NTT = T // 128
    nc = bass.Bass("TRN2", target_bir_lowering=False)

    def dram(name, shape, dt, kind="Internal"):
        return nc.dram_tensor(name, shape, dt, kind=kind).ap()

    xT_in = dram("xT", [D, T], F32, "ExternalInput")
    wst = dram("wst", [L * NT * 128, 1024], F32, "ExternalInput")
    wcf = dram("wcf", [L * 128, 128], F32, "ExternalInput")
    vec_d = dram("vec", [128, L * NVL], F32, "ExternalInput")
    outT = dram("outT", [D, T], F32, "ExternalOutput")
    wbf = dram("wbf", [L * NT * 128, 1024], BF16)
    xs = [dram("xs%d" % i, [D, T], F32) for i in range(2)]
    qn_d = dram("qn_d", [D, T], BF16)
    kn_d = dram("kn_d", [D, T], BF16)
    sc_d = dram("sc_d", [D, T], BF16)
    v_d = dram("v_d", [T, 1536], BF16)
    yc_d = dram("yc_d", [D, T], BF16)
    gC_d = dram("gC_d", [D, T], F32)
    mAB_d = dram("mAB_d", [D, T], F32)
    F_d = dram("F_d", [16, T], F32)
    Fs_d = dram("Fs_d", [16, 3, T], BF16)
    dbg_d = {}
    if dbg:
        dbg_w0 = dram("dbg_w0", [8 * 128, 1024], F32, "ExternalOutput")
        dbg_w1 = dram("dbg_w1", [8 * 128, 1024], F32, "ExternalOutput")
        for n in ("ya", "yb", "yc"):
            dbg_d[n] = dram("dbg_" + n, [D, T], F32, "ExternalOutput")

    fm = lambda ap: ap.rearrange("(k p) t -> p k t", p=128)

    with ExitStack() as st:
        P = Prog(nc, st)
        off = [16384]

        def sb_at(name, shape, dt, o=None):
            nb = int(np.prod(shape[1:])) * (4 if dt == F32 else 2)
            nb = (nb + 31) // 32 * 32
            if o is None:
                o = off[0]
                off[0] += nb
            assert o + nb <= 229120, (name, o, nb)
            return nc.alloc_sbuf_tensor_at(name, list(shape), dt, offset=o)

        psum = [st.enter_context(nc.psum_tensor("ps%d" % i, [128, 512], F32)) for i in range(7)]
        psb = [P.buf("ps%d" % i) for i in range(7)]
        pstr = st.enter_context(nc.psum_tensor("pstr", [128, 128], BF16))
        pstr_b = P.buf("pstr")
        prot = [0]

        prot_n = [5]

        def ps_rot(n=None):
            n = prot_n[0] if n is None else n
            i = prot[0] % n
            prot[0] += 1
            return psum[i], psb[i]

        vec = sb_at("vec", [128, L * NVL], F32)
        vec_b = P.buf("vec", dma=True)
        ones_f = sb_at("ones_f", [128, 128], F32)
        blk_f = sb_at("blk_f", [128, 128], F32)
        id_f = sb_at("id_f", [128, 128], F32)
        id_b = sb_at("id_b", [128, 128], BF16)
        onesK = sb_at("onesK", [128, 128], BF16)
        maskB = sb_at("maskB", [128, 64], F32)
        rmask = sb_at("rmask", [128, 512], BF16)
        onesrow = sb_at("onesrow", [128, 512], BF16)
        lbt = sb_at("lbt", [128, L, 3, 8], F32)
        cb = P.buf("consts")
        wcf_b = sb_at("wcf_b", [128, L, 128], BF16)
        wcf_buf = P.buf("wcf_b", dma=True)
        cw_b = sb_at("cw_b", [128, L, 8, CONV_K], BF16)
        P.dma("sync", vec[:], vec_d, writes=[vec_b])
        for l in range(L):
            P.dma("gpsimd", wcf_b[:, l, :], wcf[l * 128:(l + 1) * 128, :], writes=[wcf_buf])
        G = lambda fn, R=(), W=(): P.op("gpsimd", fn, reads=R, writes=W)
        G(lambda e: e.memset(ones_f[:], 1.0), W=[cb])
        G(lambda e: e.memset(blk_f[:], 0.0), W=[cb])
        G(lambda e: e.memset(blk_f[0:64, 0:64], 1.0), W=[cb])
        G(lambda e: e.memset(blk_f[64:128, 64:128], 1.0), W=[cb])
        G(lambda e: e.memset(id_f[:], 1.0), W=[cb])
        G(lambda e: e.affine_select(out=id_f[:], in_=id_f[:], pattern=[[-1, 128]], compare_op=ALU.is_equal,
                                    fill=0.0, base=0, channel_multiplier=1), R=[cb], W=[cb])
        G(lambda e: e.tensor_copy(out=id_b[:], in_=id_f[:]), R=[cb], W=[cb])
        G(lambda e: e.memset(onesK[:], 0.0), W=[cb])
        G(lambda e: e.memset(onesK[0:3, :], 1.0), W=[cb])
        G(lambda e: e.memset(maskB[:], 1.0), W=[cb])
        G(lambda e: e.affine_select(out=maskB[0:64, :], in_=maskB[0:64, :], pattern=[[1, 64]], compare_op=ALU.is_ge,
                                    fill=0.0, base=0, channel_multiplier=-1), R=[cb], W=[cb])
        G(lambda e: e.affine_select(out=maskB[64:128, :], in_=maskB[64:128, :], pattern=[[1, 64]], compare_op=ALU.is_ge,
                                    fill=0.0, base=0, channel_multiplier=-1), R=[cb], W=[cb])
        maskN = sb_at("maskN", [128, 128], BF16)
        G(lambda e: e.memset(maskN[:], -30000.0), W=[cb])
        G(lambda e: e.affine_select(out=maskN[:], in_=maskN[:], pattern=[[-1, 128]], compare_op=ALU.is_ge,
                                    fill=0.0, base=-1, channel_multiplier=1), R=[cb], W=[cb])
        G(lambda e: e.memset(rmask[:], -1.0), W=[cb])
        G(lambda e: e.memset(rmask[:].rearrange("p (c t) -> p c t", t=64)[:, :, 0:1], 0.0), W=[cb])
        G(lambda e: e.memset(onesrow[:], 1.0), W=[cb])
        Eall = sb_at("Eall", [128, L, 8], F32)
        Esum = sb_at("Esum", [128, 8], F32)
        Ecum = sb_at("Ecum", [128, 8], F32)
        for l in range(L):
            P.op("scalar", lambda e, l=l: e.activation(out=Eall[:, l, :], in_=vec[:, l * NVL + V_LBL:l * NVL + V_LBL + 8],
                                                       func=AF.Exp), reads=[vec_b], writes=[cb])
        VV = lambda fn, R=(), W=(): P.op("vector", fn, reads=R, writes=W)
        VV(lambda e: e.tensor_copy(out=Esum[:], in_=Eall[:, 0, :]), R=[cb], W=[cb])
        for l in range(1, L):
            VV(lambda e, l=l: e.tensor_tensor(out=Esum[:], in0=Esum[:], in1=Eall[:, l, :], op=ALU.add), R=[cb], W=[cb])
        VV(lambda e: e.reciprocal(out=Esum[:], in_=Esum[:]), R=[cb], W=[cb])
        VV(lambda e: e.memset(Ecum[:], 0.0), W=[cb])
        for l in range(L):
            if l > 0:
                VV(lambda e, l=l: e.tensor_tensor(out=Ecum[:], in0=Ecum[:], in1=Eall[:, l, :], op=ALU.add), R=[cb], W=[cb])
            VV(lambda e, l=l: e.tensor_tensor(out=lbt[:, l, 0, :], in0=Ecum[:], in1=Esum[:], op=ALU.mult), R=[cb], W=[cb])
            VV(lambda e, l=l: e.tensor_scalar(out=lbt[:, l, 1, :], in0=lbt[:, l, 0, :], scalar1=-1.0, scalar2=1.0,
                                              op0=ALU.mult, op1=ALU.add), R=[cb], W=[cb])
            VV(lambda e, l=l: e.tensor_scalar(out=lbt[:, l, 2, :], in0=lbt[:, l, 1, :], scalar1=-1.0, scalar2=None,
                                              op0=ALU.mult), R=[cb], W=[cb])
            VV(lambda e, l=l: e.tensor_copy(
                out=cw_b[:, l, :, :],
                in_=vec[:, l * NVL + V_CW:l * NVL + V_CW + 8 * CONV_K].rearrange("p (k j) -> p k j", j=CONV_K)),
               R=[vec_b], W=[cb])
        CONST_END = off[0]

        cvb = {}
        CV = 48
        for l in range(L):
            sizes = [4, 4, 8, 16, 16, 48, 48] if l == 0 else [CV] * (NT // CV)
            j0 = 0
            for sz in sizes:
                b = P.buf("cv%d_%d" % (l, j0), dma=True)
                r0 = (l * NT + j0) * 128
                P.dma("gpsimd", wbf[r0:r0 + sz * 128, :], wst[r0:r0 + sz * 128, :], writes=[b])
                for j in range(j0, j0 + sz):
                    cvb[(l, j // 4)] = b
                j0 += sz
            assert j0 == NT

        NWB = 3

        class WStream:
            def __init__(self):
                self.seq = []
                self.pos = 0
                self.loaded = 0
                self.bufs = None

            def set_bufs(self, tiles, bufs):
                self.tiles, self.bufs = tiles, bufs

            def _load(self, gi):
                l, g = self.seq[gi]
                t, b = self.tiles[gi % NWB], self.bufs[gi % NWB]
                r0 = (l * NT + g * 4) * 128
                src = wbf[r0:r0 + 512, :].rearrange("(j p) n -> p j n", p=128)
                P.dma("sync", t[:].rearrange("p j k n -> p j (k n)"), src, reads=[cvb[(l, g)]], writes=[b])

            def _ensure(self, gi, ahead=NWB - 1):
                while self.loaded <= min(gi + ahead, len(self.seq) - 1):
                    self._load(self.loaded)
                    self.loaded += 1

            def next(self):
                gi, j = divmod(self.pos, 4)
                self._ensure(gi)
                self.pos += 1
                return self.tiles[gi % NWB][:, j, :, :], self.bufs[gi % NWB]

            def next_group(self, ahead=NWB - 1):
                assert self.pos % 4 == 0
                gi = self.pos // 4
                self._ensure(gi, ahead)
                self.pos += 4
                return self.tiles[gi % NWB], self.bufs[gi % NWB]

        ws = WStream()
        for l in range(L):
            for ci in range(NCH):
                ws.seq += [(l, g) for g in range(P1_TILES // 4)]
            for ci in range(NCH):
                ws.seq += [(l, g) for g in range(P1_TILES // 4, NG)]

        def proj(wt, wb, rhs, rb, ps, pb):
            for k in range(KT):
                P.op("tensor", lambda e, k=k: e.matmul(ps[:], wt[:, k, :], rhs[:, k, :], start=(k == 0), stop=(k == KT - 1)),
                     reads=[wb, rb], writes=[pb], inc=(k == KT - 1))

        def act(out, in_, func, R, W, scale=1.0, bias=0.0):
            P.op("scalar", lambda e: e.activation(out=out, in_=in_, func=func, scale=scale, bias=bias), reads=R, writes=W)

        def tt(en, out, in0, in1, op, R, W):
            P.op(en, lambda e: e.tensor_tensor(out=out, in0=in0, in1=in1, op=op), reads=R, writes=W)

        def ts(en, out, in0, s1, op0, R, W, s2=None, op1=None):
            if op1 is None:
                P.op(en, lambda e: e.tensor_scalar(out=out, in0=in0, scalar1=s1, scalar2=None, op0=op0), reads=R, writes=W)
            else:
                P.op(en, lambda e: e.tensor_scalar(out=out, in0=in0, scalar1=s1, scalar2=s2, op0=op0, op1=op1),
                     reads=R, writes=W)

        def stt(out, in0, scalar, in1, op0, op1, R, W):
            P.op("vector", lambda e: e.scalar_tensor_tensor(out=out, in0=in0, scalar=scalar, in1=in1, op0=op0, op1=op1),
                 reads=R, writes=W)

        def mm(out, lhsT, rhs, start, stop, R, W, inc=True):
            P.op("tensor", lambda e: e.matmul(out, lhsT, rhs, start=start, stop=stop), reads=R, writes=W, inc=inc)

        class Pool_:
            def __init__(self, name, n, shape, dt, dma=False):
                self.t = [sb_at("%s%d" % (name, i), shape, dt) for i in range(n)]
                self.b = [P.buf("%s%d" % (name, i), dma=dma) for i in range(n)]
                self.i = 0

            def get(self):
                i = self.i % len(self.t)
                self.i += 1
                return self.t[i], self.b[i]

        def rstd_from(ps_ap, pb, scale, tmp, out=None):
            lnv, lb_ = tmp.get()
            act(lnv[:], ps_ap, AF.Ln, [pb, cb], [lb_], scale=scale, bias=epsb[:, 0:1])
            r, rb = out if out is not None else tmp.get()
            act(r[:], lnv[:], AF.Exp, [lb_], [rb], scale=-0.5)
            return r, rb

        wtiles = [sb_at("wt%d" % i, [128, 4, 8, 128], BF16) for i in range(NWB)]
        wbufs = [P.buf("wb%d" % i, dma=True) for i in range(NWB)]
        ws.set_bufs(wtiles, wbufs)
        epsb = sb_at("epsb", [128, 1], F32)
        G(lambda e: e.memset(epsb[:], EPS), W=[cb])
        CONST_END = off[0]

        for l in range(L):
            xsrc = xT_in if l == 0 else xs[(l - 1) % 2]
            xdst = outT if l == L - 1 else xs[l % 2]
            VB = l * NVL
            vcol = lambda c: vec[:, VB + c:VB + c + 1]

            if l > 0:
                P.barrier()
            if l == 0 and dbg:
                dwb = P.buf("dwb", dma=True)
                P.dma("gpsimd", dbg_w0, wbf[0:1024, :], writes=[dwb])
                P.dma("gpsimd", dbg_w1, wbf[40 * 128:48 * 128, :], writes=[dwb])
            if l == 0:
                off[0] = CONST_END
                bigf = Pool_("bigf", 3, [128, 8, 512], F32, dma=True)
                stage = Pool_("stg", 2, [128, 8, 512], BF16, dma=True)
                hTp = Pool_("hT", 2, [128, 8, 512], BF16)
                tmp1 = Pool_("tmp", 7, [128, 512], F32, dma=True)
                u_t = sb_at("u", [128, 8, 544], BF16)
                u_b = [P.buf("u%d" % c) for c in range(8)]
                dgs = [sb_at("dg%d" % i, [128, CONV_K, 128], BF16) for i in range(2)]
                dg_bs = [P.buf("dg%d" % i) for i in range(2)]
                Qt = sb_at("Qt", [128, 8, 512], BF16)
                Kt = sb_at("Kt", [128, 8, 512], BF16)
                Ktt = sb_at("Ktt", [128, 8, 512], BF16)
                QK_b = [P.buf("QK%d" % h) for h in range(8)]
                vtk = sb_at("vtk", [128, 4, 1024], BF16)
                vtk_b = P.buf("vtk", dma=True)
                st_f = sb_at("st", [128, 8, 128], F32)
                st_h = sb_at("sth", [128, 8, 128], BF16)
                st_b = [P.buf("st%d" % h) for h in range(8)]
                ebl = sb_at("ebl", [128, 8, 8], F32)
                ebl_b = [P.buf("ebl%d" % h) for h in range(8)]
                kttok = Pool_("ktk", 8, [128, 128], BF16)
                scT = Pool_("scT", 8, [128, 64], BF16)
                Fc_b = P.buf("Fc", dma=True)
                Fcar = sb_at("Fcar", [16, 1], F32)
                Fsp = sb_at("Fsp", [16, 3, 512], BF16)
                Fsp_b = P.buf("Fsp", dma=True)
                vstp = Pool_("vst", 1, [128, 8, 3, 64], BF16, dma=True)
                mean = sb_at("meanA", [128, 512], F32)
                meanb = P.buf("meanA")
                rsA_t = sb_at("rsA", [128, 512], F32)
                rsA_b = P.buf("rsA")
                P1_END = off[0]
            tmp = tmp1
            G(lambda e: e.memset(u_t[:, :, 0:32], 0.0), W=u_b)
            G(lambda e: e.memset(st_f[:], 0.0), W=st_b)
            G(lambda e: e.memset(st_h[:], 0.0), W=st_b)
            G(lambda e: e.memset(Fcar[:], 0.0), W=[Fc_b])
            for t_, b_ in zip(vstp.t, vstp.b):
                G(lambda e, t_=t_: e.memset(t_[:, :, 1, :], 1.0), W=[b_])

            def norm_chunk(ci_):
                tsl_ = slice(ci_ * 512, ci_ * 512 + 512)
                xc, xcb = bigf.get()
                P.dma("sync", xc[:], fm(xsrc)[:, :, tsl_], writes=[xcb])
                hT_, hb_ = hTp.get()
                psS, psSb = ps_rot()
                for k in range(KT):
                    sq, sqb = tmp.get()
                    act(sq[:], xc[:, k, :], AF.Square, [xcb], [sqb])
                    mm(psS[:], ones_f[:], sq[:], k == 0, k == KT - 1, [cb, sqb], [psSb])
                rs, rsb = rstd_from(psS[:], psSb, 1.0 / D, tmp)
                for k in range(KT):
                    stt(hT_[:, k, :], xc[:, k, :], vcol(V_NG + k), rs[:], ALU.mult, ALU.mult, [xcb, rsb, vec_b], [hb_])
                return hT_, hb_

            nxt = norm_chunk(0)
            for ci in range(NCH):
                t0 = ci * 512
                tsl = slice(t0, t0 + 512)
                hT, hb = nxt

                prot_n[0] = 5
                y, yb_ = bigf.get()
                S1, S1b = psum[5], psb[5]
                S2, S2b = psum[6], psb[6]
                def A1(c):
                    wv, wvb = ws.next()
                    wg, wgb = ws.next()
                    pv, pvb = ps_rot()
                    proj(wv, wvb, hT, hb, pv, pvb)
                    pg, pgb = ps_rot()
                    proj(wg, wgb, hT, hb, pg, pgb)
                    sg, sgb = tmp.get()
                    act(sg[:], pg[:], AF.Sigmoid, [pgb], [sgb])
                    tt("vector", u_t[:, c, 32:544], pv[:], sg[:], ALU.mult, [pvb, sgb], [u_b[c]])
                    dg, dg_b = dgs[c % 2], dg_bs[c % 2]
                    P.op("gpsimd", lambda e, c=c, l=l, dg=dg: e.tensor_tensor(
                        out=dg[:], in0=id_b[:].unsqueeze(1).to_broadcast([128, CONV_K, 128]),
                        in1=cw_b[:, l, c, :].unsqueeze(2).to_broadcast([128, CONV_K, 128]), op=ALU.mult),
                        reads=[cb], writes=[dg_b])

                def A2(c):
                    dg, dg_b = dgs[c % 2], dg_bs[c % 2]
                    py, pyb = ps_rot()
                    for j in range(CONV_K):
                        mm(py[:], dg[:, j, :], u_t[:, c, 2 + j:2 + j + 512], j == 0, j == CONV_K - 1,
                           [dg_b, u_b[c]], [pyb], inc=(j == CONV_K - 1))
                    act(y[:, c, :], py[:], AF.Identity, [pyb, vec_b], [yb_], bias=vcol(V_CB + c))
                    ysq, ysqb = tmp.get()
                    act(ysq[:], py[:], AF.Square, [pyb, vec_b], [ysqb], bias=vcol(V_CB + c))
                    mm(S1[:], ones_f[:], y[:, c, :], c == 0, c == 7, [cb, yb_], [S1b])
                    mm(S2[:], ones_f[:], ysq[:], c == 0, c == 7, [cb, ysqb], [S2b])

                for c in range(8):
                    A1(c)
                    if c > 0:
                        A2(c - 1)
                A2(7)
                P.op("gpsimd", lambda e: e.tensor_copy(out=u_t[:, :, 2:32], in_=u_t[:, :, 514:544]), reads=u_b, writes=u_b)
                act(mean[:], S1[:], AF.Identity, [S1b], [meanb], scale=1.0 / D)
                msq, msqb = tmp.get()
                tt("gpsimd", msq[:], mean[:], mean[:], ALU.mult, [meanb], [msqb])
                var, varb = tmp.get()
                stt(var[:], S2[:], 1.0 / D, msq[:], ALU.mult, ALU.subtract, [S2b, msqb], [varb])
                rsA, rsAb = rstd_from(var[:], varb, 1.0, tmp, out=(rsA_t, rsA_b))
                prot_n[0] = 7
                ya, yab = stage.get()
                def LN_c(c):
                        t1, t1b = tmp.get()
                        tt("vector", t1[:], y[:, c, :], mean[:], ALU.subtract, [yb_, meanb], [t1b])
                        t2, t2b = tmp.get()
                        tt("gpsimd", t2[:], t1[:], rsA[:], ALU.mult, [t1b, rsAb], [t2b])
                        t3, t3b = tmp.get()
                        ts("vector", t3[:], t2[:], vcol(V_LG + c), ALU.mult, [t2b, vec_b], [t3b], s2=vcol(V_LB + c), op1=ALU.add)
                        s3, s3b = tmp.get()
                        act(s3[:], t3[:], AF.Sigmoid, [t3b], [s3b])
                        t4, t4b = tmp.get()
                        tt("gpsimd", t4[:], t3[:], s3[:], ALU.mult, [t3b, s3b], [t4b])
                        wgt, wgtb = ws.next()
                        pgt, pgtb = ps_rot()
                        proj(wgt, wgtb, hT, hb, pgt, pgtb)
                        sgt, sgtb = tmp.get()
                        act(sgt[:], pgt[:], AF.Sigmoid, [pgtb], [sgtb])
                        t5, t5b = tmp.get()
                        tt("vector", t5[:], pgt[:], sgt[:], ALU.mult, [pgtb, sgtb], [t5b])
                        tt("vector", ya[:, c, :], t4[:], t5[:], ALU.mult, [t4b, t5b], [yab])

                def BH(h):
                        wq, wqb = ws.next()
                        pq, pqb = ps_rot()
                        proj(wq, wqb, hT, hb, pq, pqb)
                        wf, wfb = ws.next()
                        pf, pfb = ps_rot()
                        proj(wf, wfb, hT, hb, pf, pfb)
                        s_q, s_qb = tmp.get()
                        act(s_q[:], pq[:], AF.Sigmoid, [pqb], [s_qb])
                        qf, qfb = tmp.get()
                        tt("vector", qf[:], pq[:], s_q[:], ALU.mult, [pqb, s_qb], [qfb])
                        s_f, s_fb = tmp.get()
                        act(s_f[:], pf[:], AF.Sigmoid, [pfb], [s_fb])
                        kk, kkb = tmp.get()
                        ts("vector", kk[:], s_f[:], lbt[:, l, 2, h:h + 1], ALU.mult, [s_fb, cb], [kkb],
                           s2=lbt[:, l, 1, h:h + 1], op1=ALU.add)
                        d0, d0b = tmp.get()
                        stt(d0[:], kk[:], 1.0, rmask[:], ALU.subtract, ALU.mult, [kkb, cb], [d0b])
                        d1, d1b = tmp.get()
                        P.op("gpsimd", lambda e, d1=d1: e.memset(d1[:], 0.0), writes=[d1b])
                        P.op("gpsimd", lambda e, d1=d1, kk=kk: e.tensor_scalar(
                            out=d1[:].rearrange("p (c t) -> p c t", t=64)[:, :, 0:1],
                            in0=kk[:].rearrange("p (c t) -> p c t", t=64)[:, :, 0:1],
                            scalar1=-1.0, scalar2=1.0, op0=ALU.mult, op1=ALU.add), reads=[kkb], writes=[d1b])
                        eb, ebb = tmp.get()
                        P.op("vector", lambda e, eb=eb, d0=d0, d1=d1: e.tensor_tensor_scan(
                            out=eb[:], data0=d0[:], data1=d1[:], initial=0.0, op0=ALU.mult, op1=ALU.add),
                            reads=[d0b, d1b], writes=[ebb])
                        tt("vector", Qt[:, h, :], qf[:], eb[:], ALU.mult, [qfb, ebb], [QK_b[h]])
                        enb, enbb = tmp.get()
                        P.op("vector", lambda e, eb=eb, enb=enb: e.reciprocal(out=enb[:], in_=eb[:]),
                             reads=[ebb], writes=[enbb])
                        tt("gpsimd", Kt[:, h, :], kk[:], enb[:], ALU.mult, [kkb, enbb], [QK_b[h]])
                        ebv = eb[:].rearrange("p (c t) -> p c t", t=64)[:, :, 63:64]
                        P.op("vector", lambda e, h=h, ebv=ebv: e.tensor_tensor(
                            out=Ktt[:, h, :].rearrange("p (c t) -> p c t", t=64),
                            in0=Kt[:, h, :].rearrange("p (c t) -> p c t", t=64),
                            in1=ebv.to_broadcast([128, 8, 64]), op=ALU.mult),
                            reads=[QK_b[h], ebb], writes=[QK_b[h]])
                        P.op("gpsimd", lambda e, h=h, ebv=ebv: e.tensor_copy(out=ebl[:, h, :].unsqueeze(2), in_=ebv),
                             reads=[ebb], writes=[ebl_b[h]])

                for i_ in range(8):
                    LN_c(i_)
                    BH(i_)
                if dbg:
                    yaf, yafb = bigf.get()
                    VV(lambda e: e.tensor_copy(out=yaf[:], in_=ya[:]), R=[yab], W=[yafb])
                    P.dma("sync", fm(dbg_d["ya"])[:, :, tsl], yaf[:], reads=[yafb], writes=[P.buf("dd")])
                if ci + 1 < NCH:
                    nxt = norm_chunk(ci + 1)
                mAB, mABb = bigf.get()
                for d in range(8):
                    wb_, wbb = ws.next()
                    wm, wmb = ws.next()
                    pd, pdb = ps_rot()
                    proj(wb_, wbb, ya, yab, pd, pdb)
                    pm, pmb = ps_rot()
                    proj(wm, wmb, hT, hb, pm, pmb)
                    gm, gmb = tmp.get()
                    act(gm[:], pm[:], AF.Sigmoid, [pmb], [gmb])
                    tt("vector", mAB[:, d, :], pd[:], gm[:], ALU.mult, [pdb, gmb], [mABb])

                o_sb, ob = bigf.get()
                for half in range(2):
                    wgp, wgpb = ws.next_group()
                    for tq in range(4):
                        pv, pvb = ps_rot()
                        for k in range(KT):
                            mm(pv[:], hT[:, k, tq * 128:(tq + 1) * 128], wgp[:, :, k, :], k == 0, k == KT - 1,
                               [hb, wgpb], [pvb], inc=(k == KT - 1))
                        act(vtk[:, tq, half * 512:(half + 1) * 512], pv[:], AF.Identity, [pvb], [vtk_b])
                for tq in range(4):
                    c0 = tq * 128
                    kt_l, sT_l = [], []
                    for h in range(8):
                        P.op("tensor", lambda e, h=h, c0=c0: e.transpose(pstr[:], Ktt[:, h, c0:c0 + 128], id_b[:]),
                             reads=[QK_b[h], cb], writes=[pstr_b])
                        ktk, ktkb = kttok.get()
                        VV(lambda e, ktk=ktk: e.tensor_copy(out=ktk[:], in_=pstr[:]), R=[pstr_b], W=[ktkb])
                        pss, pssb = ps_rot()
                        for j in range(2):
                            cj = c0 + j * 64
                            mm(pss[j * 64:(j + 1) * 64, 0:64], Kt[:, h, cj:cj + 64], Qt[:, h, cj:cj + 64], True, True,
                               [QK_b[h]], [pssb], inc=(j == 1))
                        sT, sTb = scT.get()
                        tt("vector", sT[:], pss[:, 0:64], maskB[:], ALU.mult, [pssb, cb], [sTb])
                        kt_l.append((ktk, ktkb))
                        sT_l.append((sT, sTb))
                    for j in range(2):
                        cj = c0 + j * 64
                        r = slice(j * 64, j * 64 + 64)
                        for h in range(8):
                            ktk, ktkb = kt_l[h]
                            sT, sTb = sT_l[h]
                            po, pob = ps_rot()
                            mm(po[:, 0:64], vtk[r, tq, h * 128:(h + 1) * 128], sT[r, :], True, False,
                               [vtk_b, sTb], [pob], inc=False)
                            mm(po[:, 0:64], st_h[:, h, :], Qt[:, h, cj:cj + 64], False, True,
                               [st_b[h], QK_b[h]], [pob], inc=True)
                            pu, pub = ps_rot()
                            mm(pu[:, 0:128], ktk[r, :], vtk[r, tq, h * 128:(h + 1) * 128], True, True,
                               [ktkb, vtk_b], [pub])
                            stt(st_f[:, h, :], st_f[:, h, :], ebl[:, h, tq * 2 + j:tq * 2 + j + 1], pu[:, 0:128],
                                ALU.mult, ALU.add, [st_b[h], ebl_b[h], pub], [st_b[h]])
                            P.op("gpsimd", lambda e, h=h: e.tensor_copy(out=st_h[:, h, :], in_=st_f[:, h, :]),
                                 reads=[st_b[h]], writes=[st_b[h]])
                            act(o_sb[:, h, cj:cj + 64], po[:, 0:64], AF.Identity, [pob], [ob])
                yB, yBb = stage.get()
                rsBall, rsBallb = bigf.get()

                def B1(h):
                    osq, osqb = tmp.get()
                    act(osq[:], o_sb[:, h, :], AF.Square, [ob], [osqb])
                    return osq, osqb

                def B2(h, osq, osqb):
                    pn, pnb = ps_rot()
                    mm(pn[:], ones_f[:], osq[:], True, True, [cb, osqb], [pnb])
                    act(rsBall[:, h, :], pn[:], AF.Identity, [pnb, cb], [rsBallb], scale=1.0 / 128, bias=epsb[:, 0:1])

                pb1 = None
                for h in range(8):
                    cur = B1(h)
                    if pb1 is not None:
                        B2(h - 1, *pb1)
                    pb1 = cur
                B2(7, *pb1)
                rsv = rsBall[:].rearrange("p k t -> p (k t)")
                act(rsv, rsv, AF.Ln, [rsBallb], [rsBallb])
                act(rsv, rsv, AF.Exp, [rsBallb], [rsBallb], scale=-0.5)
                for h in range(8):
                    wgt, wgtb = ws.next()
                    pgt, pgtb = ps_rot()
                    proj(wgt, wgtb, hT, hb, pgt, pgtb)
                    sgt, sgtb = tmp.get()
                    act(sgt[:], pgt[:], AF.Sigmoid, [pgtb], [sgtb])
                    t5, t5b = tmp.get()
                    tt("vector", t5[:], pgt[:], sgt[:], ALU.mult, [pgtb, sgtb], [t5b])
                    t1, t1b = tmp.get()
                    stt(t1[:], o_sb[:, h, :], vcol(V_HG + h), rsBall[:, h, :], ALU.mult, ALU.mult, [ob, rsBallb, vec_b], [t1b])
                    tt("gpsimd", yB[:, h, :], t1[:], t5[:], ALU.mult, [t1b, t5b], [yBb])
                if dbg:
                    yaf, yafb = bigf.get()
                    VV(lambda e: e.tensor_copy(out=yaf[:], in_=yB[:]), R=[yBb], W=[yafb])
                    P.dma("sync", fm(dbg_d["yb"])[:, :, tsl], yaf[:], reads=[yafb], writes=[P.buf("dd")])
                for d in range(8):
                    wb_, wbb = ws.next()
                    wm, wmb = ws.next()
                    pd, pdb = ps_rot()
                    proj(wb_, wbb, yB, yBb, pd, pdb)
                    pm, pmb = ps_rot()
                    proj(wm, wmb, hT, hb, pm, pmb)
                    gm, gmb = tmp.get()
                    act(gm[:], pm[:], AF.Sigmoid, [pmb], [gmb])
                    t6, t6b = tmp.get()
                    tt("vector", t6[:], pd[:], gm[:], ALU.mult, [pdb, gmb], [t6b])
                    tt("gpsimd", mAB[:, d, :], mAB[:, d, :], t6[:], ALU.add, [mABb, t6b], [mABb])
                P.dma("gpsimd", fm(mAB_d)[:, :, tsl], mAB[:], reads=[mABb], writes=[P.buf("mABd%d" % ci)])

                for which, dst in ((0, qn_d), (1, kn_d)):
                    qn, qnb = stage.get()
                    def C1(pr):
                        wq, wqb = ws.next()
                        pq, pqb = ps_rot()
                        proj(wq, wqb, hT, hb, pq, pqb)
                        qsq, qsqb = tmp.get()
                        act(qsq[:], pq[:], AF.Square, [pqb], [qsqb])
                        return pq, pqb, qsq, qsqb

                    def C2(pr, pq, pqb, qsq, qsqb, which=which, qn=qn, qnb=qnb):
                        pn, pnb = ps_rot()
                        mm(pn[:], blk_f[:], qsq[:], True, True, [cb, qsqb], [pnb])
                        rsC, rsCb = rstd_from(pn[:], pnb, 1.0 / 64, tmp)
                        stt(qn[:, pr, :], pq[:], vcol(V_QG if which == 0 else V_KG), rsC[:], ALU.mult, ALU.mult,
                            [pqb, rsCb, vec_b], [qnb])

                    pc1 = None
                    for pr in range(8):
                        cur = C1(pr)
                        if pc1 is not None:
                            C2(pr - 1, *pc1)
                        pc1 = cur
                    C2(7, *pc1)
                    P.dma("gpsimd", fm(dst)[:, :, tsl], qn[:], reads=[qnb], writes=[P.buf("qkd")])
                wgps = [ws.next_group(), ws.next_group(ahead=1)]
                for tq in range(4):
                    vs_, vsb = vstp.get()
                    for half in range(2):
                        wgp, wgpb = wgps[half]
                        pv, pvb = ps_rot()
                        for k in range(KT):
                            mm(pv[:], hT[:, k, tq * 128:(tq + 1) * 128], wgp[:, :, k, :], k == 0, k == KT - 1,
                               [hb, wgpb], [pvb], inc=(k == KT - 1))
                        act(vs_[:, half * 4:(half + 1) * 4, 0:3:2, :], pv[:].rearrange("p (a b c) -> p a b c", b=2, c=64),
                            AF.Identity, [pvb], [vsb])
                    P.dma("gpsimd", v_d[t0 + tq * 128:t0 + (tq + 1) * 128, :], vs_[:].rearrange("p a b c -> p (a b c)"),
                          reads=[vsb], writes=[P.buf("vd")])
                sC, sCb = stage.get()
                for pr in range(8):
                    wgt, wgtb = ws.next()
                    pgt, pgtb = ps_rot()
                    proj(wgt, wgtb, hT, hb, pgt, pgtb)
                    sgt, sgtb = tmp.get()
                    act(sgt[:], pgt[:], AF.Sigmoid, [pgtb], [sgtb])
                    tt("vector", sC[:, pr, :], pgt[:], sgt[:], ALU.mult, [pgtb, sgtb], [sCb])
                P.dma("gpsimd", fm(sc_d)[:, :, tsl], sC[:], reads=[sCb], writes=[P.buf("scd")])
                gC, gCb = bigf.get()
                for d in range(8):
                    wm, wmb = ws.next()
                    pm, pmb = ps_rot()
                    proj(wm, wmb, hT, hb, pm, pmb)
                    act(gC[:, d, :], pm[:], AF.Sigmoid, [pmb], [gCb])
                P.dma("gpsimd", fm(gC_d)[:, :, tsl], gC[:], reads=[gCb], writes=[P.buf("gCd")])
                pF, pFb = ps_rot()
                for k in range(KT):
                    mm(pF[0:16, :], wcf_b[:, l, k * 16:(k + 1) * 16], hT[:, k, :], k == 0, k == KT - 1,
                       [wcf_buf, hb], [pFb], inc=(k == KT - 1))
                fa, fab = tmp.get()
                fbx, fbxb = tmp.get()
                Fc_t, FcB = tmp.get()
                Fc = Fc_t[0:16, :]
                act(fa[0:16, :], pF[0:16, :], AF.Sigmoid, [pFb, vec_b], [fab], bias=vec[0:16, VB + V_FB:VB + V_FB + 1])
                act(fbx[0:16, :], fa[0:16, :], AF.Ln, [fab], [fbxb])
                P.op("vector", lambda e, fbx=fbx, Fc=Fc: e.tensor_tensor_scan(out=Fc, data0=onesrow[0:16, :], data1=fbx[0:16, :],
                                                              initial=Fcar[:, 0:1], op0=ALU.mult, op1=ALU.add),
                     reads=[cb, fbxb, Fc_b], writes=[FcB])
                VV(lambda e, Fc=Fc: e.tensor_copy(out=Fcar[:], in_=Fc[:, 511:512]), R=[FcB], W=[Fc_b])
                P.dma("gpsimd", F_d[:, tsl], Fc, reads=[FcB], writes=[P.buf("Fd")])
                ts("vector", fa[0:16, :], Fc, 8.0, ALU.mult, [FcB], [fab])
                VV(lambda e, fa=fa: e.tensor_copy(out=Fsp[:, 0, :], in_=fa[0:16, :]), R=[fab], W=[Fsp_b])
                tt("vector", fbx[0:16, :], fa[0:16, :], Fsp[:, 0, :], ALU.subtract, [fab, Fsp_b], [fbxb])
                VV(lambda e, fbx=fbx: e.tensor_copy(out=Fsp[:, 1, :], in_=fbx[0:16, :]), R=[fbxb], W=[Fsp_b])
                tt("vector", fa[0:16, :], fbx[0:16, :], Fsp[:, 1, :], ALU.subtract, [fbxb, Fsp_b], [fab])
                VV(lambda e, fa=fa: e.tensor_copy(out=Fsp[:, 2, :], in_=fa[0:16, :]), R=[fab], W=[Fsp_b])
                P.dma("gpsimd", Fs_d[:, :, tsl], Fsp[:], reads=[Fsp_b], writes=[P.buf("Fsd")])

            P.barrier()
            if l == 0:
                off[0] = CONST_END
                Kp = Pool_("Kp", 2, [128, T], BF16, dma=True)
                Vp = Pool_("Vp", 2, [128, NTT, 192], BF16, dma=True)
                negF = sb_at("negF", [128, NTT, 16], F32)
                negF_b = P.buf("negF")
                FT = sb_at("FT", [16, 512], F32)
                FT_b = P.buf("FT", dma=True)
                qzp = [Pool_("qz%d_" % i, 2, [128, 512], BF16, dma=True) for i in range(2)]
                fbp = Pool_("fbp", 4, [128, 512], BF16, dma=True)
                scp = Pool_("scp", 2, [128, 512], BF16, dma=True)
                ptp = Pool_("ptp", 4, [128, 512], BF16)
                ycg = Pool_("ycg", 2, [128, 512], BF16, dma=True)
                tmp2 = Pool_("tm2", 8, [128, 512], F32, dma=True)
                ycT = sb_at("ycT", [128, 8, 512], BF16)
                ycb = P.buf("ycT", dma=True)
                mixb = sb_at("mixb", [128, 8, 512], BF16)
                mixbb = P.buf("mixb")
                P2_END = off[0]
            tmp = tmp2
            for t_, b_ in zip(fbp.t + qzp[0].t + qzp[1].t, fbp.b + qzp[0].b + qzp[1].b):
                G(lambda e, t_=t_: e.memset(t_[:], 0.0), W=[b_])
            for ci in range(NCH):
                tsl = slice(ci * 512, ci * 512 + 512)
                P.dma("sync", FT[:], F_d[:, tsl], writes=[FT_b])
                for a_ in range(4):
                    tq = ci * 4 + a_
                    pT, pTb = ps_rot(3)
                    P.op("tensor", lambda e, a_=a_, pT=pT: e.transpose(pT[:, 0:16], FT[:, a_ * 128:(a_ + 1) * 128], id_f[0:16, 0:16]),
                         reads=[FT_b, cb], writes=[pTb])
                    ts("vector", negF[:, tq, :], pT[:, 0:16], -1.0, ALU.mult, [pTb], [negF_b])

            Fs_v = Fs_d.rearrange("(pr par) r t -> par r pr t", par=2)
            for pr in range(8):
                Kt_, Ktb = Kp.get()
                P.dma("sync", Kt_[:], kn_d[pr * 128:(pr + 1) * 128, :], writes=[Ktb])
                Vt_, Vtb = Vp.get()
                P.dma("sync", Vt_[:], v_d[:, pr * 192:(pr + 1) * 192].rearrange("(a p) n -> p a n", p=128), writes=[Vtb])
                for ci in range(NCH):
                    t0 = ci * 512
                    tsl = slice(t0, t0 + 512)
                    qz, fbt = [], []
                    for par in range(2):
                        r = slice(par * 64, par * 64 + 64)
                        qt_, qtb = qzp[par].get()
                        P.dma("sync", qt_[r, :], fm(qn_d)[r, pr, tsl], writes=[qtb])
                        qz.append((qt_, qtb))
                        fb_, fbb = fbp.get()
                        P.dma("sync", fb_[0:3, :], Fs_v[par, :, pr, tsl], writes=[fbb])
                        fbt.append((fb_, fbb))
                    sct, sctb = scp.get()
                    P.dma("sync", sct[:], fm(sc_d)[:, pr, tsl], writes=[sctb])
                    par_ = (pr * NCH + ci) % 2
                    Oacc = [(psum[3 + 2 * par_], psb[3 + 2 * par_]), (psum[4 + 2 * par_], psb[4 + 2 * par_])]
                    nkt = 4 * (ci + 1)
                    blocks = [(kt, par) for kt in range(nkt) for par in range(2)]

                    def qk(i):
                        kt, par = blocks[i]
                        q_lo = max(0, kt - 4 * ci) * 128
                        pS, pSb = ps_rot(3)
                        mm(pS[:, q_lo:512], Kt_[:, kt * 128:(kt + 1) * 128], qz[par][0][:, q_lo:512], True, False,
                           [Ktb, qz[par][1]], [pSb], inc=False)
                        diag = (kt - 4 * ci) >= 0
                        mm(pS[:, q_lo:512], onesK[:], fbt[par][0][:, q_lo:512], False, not diag, [cb, fbt[par][1]], [pSb],
                           inc=not diag)
                        if diag:
                            mm(pS[:, q_lo:q_lo + 128], id_b[:], maskN[:], False, True, [cb], [pSb])
                        return pS, pSb

                    def pv(i, pS, pSb):
                        kt, par = blocks[i]
                        a = kt - 4 * ci
                        q_lo = max(0, a) * 128
                        h = 2 * pr + par
                        pt, ptb = ptp.get()
                        act(pt[:, q_lo:512], pS[:, q_lo:512], AF.Exp, [pSb, negF_b], [ptb], scale=0.125,
                            bias=negF[:, kt, h:h + 1])
                        O, Ob = Oacc[par]
                        mm(O[:, q_lo:512], Vt_[:, kt, par * 64:par * 64 + 128], pt[:, q_lo:512], kt == 0, kt == nkt - 1,
                           [Vtb, ptb], [Ob])

                    DEPTH_ = 2
                    pend = [qk(i) for i in range(min(DEPTH_, len(blocks)))]
                    for i in range(len(blocks)):
                        if i + DEPTH_ < len(blocks):
                            pend.append(qk(i + DEPTH_))
                        pS, pSb = pend.pop(0)
                        pv(i, pS, pSb)
                    (OA, OAb), (OB, OBb) = Oacc
                    R_, Rb = tmp.get()
                    VV(lambda e, R_=R_, OB=OB: e.reciprocal(out=R_[0:64, :], in_=OB[0:64, :]), R=[OBb], W=[Rb])
                    VV(lambda e, R_=R_, OA=OA: e.reciprocal(out=R_[64:128, :], in_=OA[64:128, :]), R=[OAb], W=[Rb])
                    Rs, Rsb = tmp.get()
                    P.dma("gpsimd", Rs[0:64, :], R_[64:128, :], reads=[Rb], writes=[Rsb])
                    P.dma("gpsimd", Rs[64:128, :], R_[0:64, :], reads=[Rb], writes=[Rsb])
                    t1, t1b = tmp.get()
                    tt("vector", t1[0:64, :], OA[0:64, :], Rs[0:64, :], ALU.mult, [OAb, Rsb], [t1b])
                    tt("vector", t1[64:128, :], OB[64:128, :], Rs[64:128, :], ALU.mult, [OBb, Rsb], [t1b])
                    if dbg:
                        P.dma("sync", fm(dbg_d["yc"])[:, pr, tsl], t1[:], reads=[t1b], writes=[P.buf("dd")])
                    yg, ygb = ycg.get()
                    tt("gpsimd", yg[:], t1[:], sct[:], ALU.mult, [t1b, sctb], [ygb])
                    P.dma("gpsimd", yc_d[pr * 128:(pr + 1) * 128, tsl], yg[:], reads=[ygb], writes=[P.buf("ycd")])
            P.barrier()
            for ci in range(NCH):
                t0 = ci * 512
                tsl = slice(t0, t0 + 512)
                P.dma("sync", ycT[:], fm(yc_d)[:, :, tsl], writes=[ycb])
                for d in range(8):
                    wb_, wbb = ws.next()
                    pd, pdb = ps_rot(3)
                    proj(wb_, wbb, ycT, ycb, pd, pdb)
                    gt, gtb = tmp.get()
                    P.dma("sync", gt[:], fm(gC_d)[:, d, tsl], writes=[gtb])
                    mt, mtb = tmp.get()
                    P.dma("sync", mt[:], fm(mAB_d)[:, d, tsl], writes=[mtb])
                    t2, t2b = tmp.get()
                    tt("vector", t2[:], pd[:], gt[:], ALU.mult, [pdb, gtb], [t2b])
                    tt("gpsimd", mixb[:, d, :], t2[:], mt[:], ALU.add, [t2b, mtb], [mixbb])
                for e_ in range(8):
                    wo, wob = ws.next()
                    pe, peb = ps_rot(3)
                    proj(wo, wob, mixb, mixbb, pe, peb)
                    xt_, xtb = tmp.get()
                    P.dma("sync", xt_[:], fm(xsrc)[:, e_, tsl], writes=[xtb])
                    xo, xob = tmp.get()
                    tt("vector", xo[:], pe[:], xt_[:], ALU.add, [peb, xtb], [xob])
                    P.dma("gpsimd", fm(xdst)[:, e_, tsl], xo[:], reads=[xob], writes=[P.buf("xd")])

        P.barrier()
        P.emit()
    return nc


def _tile(w_cols):
    return w_cols.reshape(8, 128, 128).transpose(1, 0, 2)


def prep_weights(L, norm_g, w_in, conv_w, conv_b, conv_ln_g, conv_ln_b, hgrn_lb_logits, hgrn_norm_g,
                 fox_f_bias, fox_qn_g, fox_kn_g, w_branch, w_out):
    off = {"a_val": 0, "a_glu": 1024, "a_gate": 2048, "b_q": 3072, "b_f": 4096, "b_i": 5120, "b_gate": 6144,
           "c_q": 7168, "c_k": 8192, "c_v": 9216, "c_gate": 10240, "c_f": 11264, "m_a": 11280, "m_b": 12304,
           "m_c": 13328}
    wst = np.empty((L, NT, 128, 8, 128), np.float32)
    wcf = np.empty((L, 128, 8, 16), np.float32)
    vec = np.zeros((128, L, NVL), np.float32)
    pk = lambda v: v.reshape(8, 128).T
    for l in range(L):
        W = w_in[l]
        it = lambda name, i: _tile(W[:, off[name] + i * 128: off[name] + (i + 1) * 128])
        bt = lambda br, i: _tile(w_branch[l, br][:, i * 128:(i + 1) * 128])
        tiles = []
        for c in range(8):
            tiles += [it("a_val", c), it("a_glu", c)]
        for i in range(8):
            tiles += [it("a_gate", i), it("b_q", i), it("b_f", i)]
        for d in range(8):
            tiles += [bt(0, d), it("m_a", d)]
        tiles += [it("b_i", i) for i in range(8)]
        tiles += [it("b_gate", h) for h in range(8)]
        for d in range(8):
            tiles += [bt(1, d), it("m_b", d)]
        tiles += [it("c_q", i) for i in range(8)]
        tiles += [it("c_k", i) for i in range(8)]
        tiles += [it("c_v", i) for i in range(8)]
        tiles += [it("c_gate", i) for i in range(8)]
        tiles += [it("m_c", i) for i in range(8)]
        tiles += [bt(2, d) for d in range(8)]
        tiles += [_tile(w_out[l][:, e * 128:(e + 1) * 128]) for e in range(8)]
        assert len(tiles) == NT
        wst[l] = np.stack(tiles)
        wcf[l] = W[:, off["c_f"]:off["c_f"] + 16].reshape(8, 128, 16).transpose(1, 0, 2)
        vec[:, l, V_NG:V_NG + 8] = pk(norm_g[l])
        vec[:, l, V_CB:V_CB + 8] = pk(conv_b[l])
        vec[:, l, V_LG:V_LG + 8] = pk(conv_ln_g[l])
        vec[:, l, V_LB:V_LB + 8] = pk(conv_ln_b[l])
        vec[:, l, V_HG:V_HG + 8] = pk(hgrn_norm_g[l])
        vec[:, l, V_LBL:V_LBL + 8] = pk(hgrn_lb_logits[l])
        vec[:, l, V_CW:V_CW + 248] = conv_w[l].reshape(CONV_K, 8, 128).transpose(2, 1, 0).reshape(128, 248)
        vec[:, l, V_QG] = np.tile(fox_qn_g[l], 2)
        vec[:, l, V_KG] = np.tile(fox_kn_g[l], 2)
        vec[:16, l, V_FB] = fox_f_bias[l]
    return (np.ascontiguousarray(wst.reshape(L * NT * 128, 1024)),
            np.ascontiguousarray(wcf.reshape(L * 128, 128)),
            np.ascontiguousarray(vec.reshape(128, L * NVL)))


_NC_CACHE = {}


def run(x, params, L, dbg=False, n_cores=None):
    B, T, _ = x.shape
    n_cores = n_cores or B
    key = (T, L, dbg)
    if key not in _NC_CACHE:
        _NC_CACHE[key] = build(T, L, dbg)
    nc = _NC_CACHE[key]
    wst, wcf, vec = prep_weights(L, **params)
    in_maps = [{"xT": np.ascontiguousarray(x[b].T), "wst": wst, "wcf": wcf, "vec": vec} for b in range(n_cores)]
    res = run_bass_kernel_spmd(nc, in_maps, core_ids=list(range(n_cores)))
    return res.results


def kernel(x, norm_g, w_in, conv_w, conv_b, conv_ln_g, conv_ln_b, hgrn_lb_logits, hgrn_norm_g,
           fox_f_bias, fox_qn_g, fox_kn_g, w_branch, w_out):
    x = np.asarray(x, np.float32)
    params = dict(norm_g=norm_g, w_in=w_in, conv_w=conv_w, conv_b=conv_b, conv_ln_g=conv_ln_g,
                  conv_ln_b=conv_ln_b, hgrn_lb_logits=hgrn_lb_logits, hgrn_norm_g=hgrn_norm_g,
                  fox_f_bias=fox_f_bias, fox_qn_g=fox_qn_g, fox_kn_g=fox_kn_g, w_branch=w_branch, w_out=w_out)
    params = {k: np.asarray(v, np.float32) for k, v in params.items()}
    L = params["w_in"].shape[0]
    results = run(x, params, L)
    out = np.stack([np.ascontiguousarray(r["outT"].T) for r in results], axis=0)
    return out.astype(np.float32)
```
